# Optimizing a Trainium2 kernel written in Bass

```python
import math
import jax, jax.numpy as jnp
from jax import lax
import numpy as np

D_MODEL = 1024
BATCH = 8
SEQ = 2048
DEPTH = 1

GRID_W = 64
MIX_WIDTH = D_MODEL
FOURIER_WIDTH = MIX_WIDTH // 2
N_FOURIER_GROUPS = 4
FG_DIM = FOURIER_WIDTH // N_FOURIER_GROUPS
NA_WIDTH = MIX_WIDTH - FOURIER_WIDTH
HEAD_DIM = 64
NA_HEADS = NA_WIDTH // HEAD_DIM
WIN_R_MAX = 8
WIN_C = 16
IN_COLS = FOURIER_WIDTH + 3 * NA_WIDTH
D_FF = 2816
CONV_W = 3
N_MOD = 6
EPS = 1e-6
NEG_INF = -1e30

kernel_name = "hybrid_fourier_natten_convffn_adaln"


def rms_norm(x, g):
    xf = x.astype(jnp.float32)
    y = xf * lax.rsqrt(jnp.mean(xf * xf, axis=-1, keepdims=True) + EPS)
    return (y * g.astype(jnp.float32)).astype(x.dtype)


def modulate(h, shift, scale):
    return h * (1.0 + scale[:, None, :]) + shift[:, None, :]


def fourier_mix(u, w_four):
    B, S, _ = u.shape
    ug = u.reshape(B, S, N_FOURIER_GROUPS, FG_DIM).astype(jnp.float32)
    f = jnp.fft.fft2(ug, axes=(1, 3), norm="ortho").real.astype(u.dtype)
    y = jnp.einsum("bsgc,gcd->bsgd", f, w_four)
    return y.reshape(B, S, FOURIER_WIDTH)


def neighbourhood_attention(q, k, v, rpb):
    B, S, _ = q.shape
    rows = S // GRID_W
    kr = min(WIN_R_MAX, rows)

    def to_grid(t):
        return t.reshape(B, rows, GRID_W, NA_HEADS, HEAD_DIM).transpose(0, 3, 1, 2, 4)

    qg, kg, vg = to_grid(q), to_grid(k), to_grid(v)

    r = jnp.arange(rows)
    row_start = jnp.clip(r - kr // 2, 0, rows - kr)
    row_idx = row_start[:, None] + jnp.arange(kr)
    kb = jnp.take(kg, row_idx, axis=2)
    vb = jnp.take(vg, row_idx, axis=2)

    cols = jnp.arange(GRID_W)
    col_start = jnp.clip(cols - WIN_C // 2, 0, GRID_W - WIN_C)
    valid = (cols[None, :] >= col_start[:, None]) & (cols[None, :] < col_start[:, None] + WIN_C)

    dr = row_idx - r[:, None]
    dc = jnp.clip(cols[None, :] - cols[:, None], -(WIN_C - 1), WIN_C - 1)
    bias = rpb.astype(jnp.float32)[:, dr + (WIN_R_MAX - 1)]
    bias = bias[..., dc + (WIN_C - 1)]
    bias = jnp.where(valid, bias, NEG_INF).transpose(0, 1, 3, 2, 4)

    scale = 1.0 / math.sqrt(HEAD_DIM)
    s = jnp.einsum("bhrqd,bhrikd->bhrqik", qg, kb).astype(jnp.float32) * scale + bias[None]
    p = jax.nn.softmax(s.reshape(s.shape[:4] + (kr * GRID_W,)), axis=-1)
    p = p.reshape(s.shape).astype(v.dtype)
    o = jnp.einsum("bhrqik,bhrikd->bhrqd", p, vb)
    return o.transpose(0, 2, 3, 1, 4).reshape(B, S, NA_WIDTH)


def dwconv_centred(u, w, b):
    up = jnp.pad(u, ((0, 0), (1, 1), (0, 0)))
    return up[:, :-2] * w[0] + up[:, 1:-1] * w[1] + up[:, 2:] * w[2] + b


def setup_inputs(seed: int = 0) -> dict:
    key = jax.random.key(seed)
    ks = jax.random.split(key, 20)
    L, D = DEPTH, D_MODEL
    nrm = jax.random.normal
    return {
        "x": nrm(ks[0], (BATCH, SEQ, D), jnp.float32),
        "c": nrm(ks[1], (BATCH, D), jnp.float32),
        "w_ada": nrm(ks[2], (L, D, N_MOD * D), jnp.float32) * (0.5 * D ** -0.5),
        "b_ada": nrm(ks[3], (L, N_MOD * D), jnp.float32) * 0.02,
        "g_mix": 1.0 + 0.02 * nrm(ks[4], (L, D), jnp.float32),
        "w_in": nrm(ks[5], (L, D, IN_COLS), jnp.float32) * D ** -0.5,
        "w_four": nrm(ks[6], (L, N_FOURIER_GROUPS, FG_DIM, FG_DIM), jnp.float32) * FG_DIM ** -0.5,
        "rpb": nrm(ks[7], (L, NA_HEADS, 2 * WIN_R_MAX - 1, 2 * WIN_C - 1), jnp.float32) * 0.1,
        "g_four_out": 1.0 + 0.02 * nrm(ks[8], (L, FOURIER_WIDTH), jnp.float32),
        "g_na_out": 1.0 + 0.02 * nrm(ks[9], (L, NA_WIDTH), jnp.float32),
        "w_out": nrm(ks[10], (L, MIX_WIDTH, D), jnp.float32) * MIX_WIDTH ** -0.5,
        "g_ffn": 1.0 + 0.02 * nrm(ks[11], (L, D), jnp.float32),
        "w_up": nrm(ks[12], (L, D, 2 * D_FF), jnp.float32) * D ** -0.5,
        "conv_w": nrm(ks[13], (L, CONV_W, 2 * D_FF), jnp.float32) * CONV_W ** -0.5,
        "conv_b": nrm(ks[14], (L, 2 * D_FF), jnp.float32) * 0.02,
        "w_down": nrm(ks[15], (L, D_FF, D), jnp.float32) * D_FF ** -0.5,
        "g_final": 1.0 + 0.02 * nrm(ks[16], (D,), jnp.float32),
    }


def reference(x, c, w_ada, b_ada, g_mix, w_in, w_four, rpb, g_four_out, g_na_out, w_out,
              g_ffn, w_up, conv_w, conv_b, w_down, g_final):
    D = D_MODEL
    cs = jax.nn.silu(c)
    for l in range(DEPTH):
        mod = cs @ w_ada[l] + b_ada[l]
        sh1, sc1, gt1, sh2, sc2, gt2 = [mod[:, i * D:(i + 1) * D] for i in range(N_MOD)]

        h = modulate(rms_norm(x, g_mix[l]), sh1, sc1)
        proj = h @ w_in[l]
        u_f = proj[..., :FOURIER_WIDTH]
        q = proj[..., FOURIER_WIDTH:FOURIER_WIDTH + NA_WIDTH]
        k = proj[..., FOURIER_WIDTH + NA_WIDTH:FOURIER_WIDTH + 2 * NA_WIDTH]
        v = proj[..., FOURIER_WIDTH + 2 * NA_WIDTH:]
        y_f = rms_norm(fourier_mix(u_f, w_four[l]), g_four_out[l])
        y_na = rms_norm(neighbourhood_attention(q, k, v, rpb[l]), g_na_out[l])
        y = jnp.concatenate([y_f, y_na], axis=-1) @ w_out[l]
        x = x + gt1[:, None, :] * y

        h2 = modulate(rms_norm(x, g_ffn[l]), sh2, sc2)
        up = dwconv_centred(h2 @ w_up[l], conv_w[l], conv_b[l])
        a = jax.nn.silu(up[..., :D_FF]) * up[..., D_FF:]
        x = x + gt2[:, None, :] * (a @ w_down[l])
    return rms_norm(x, g_final)
```

```python
import math
from contextlib import ExitStack

import numpy as np
import ml_dtypes
import concourse.bass as bass
import concourse.mybir as mybir
from concourse.bass_utils import run_bass_kernel_spmd

F32 = mybir.dt.float32
BF16 = mybir.dt.bfloat16
AF = mybir.ActivationFunctionType
ALU = mybir.AluOpType

D = 1024
S = 2048
NT = 16
DFF = 2816
NFC = 22
EPS = 1e-6
ENGS = ["pe", "act", "dve", "pool", "sp"]
DEBUG = []


class Op:
    __slots__ = ("eng", "fn", "deps", "signal", "seq", "dma", "twrites")

    def __init__(self, eng, fn, dma):
        self.eng = eng
        self.fn = fn
        self.deps = []
        self.signal = False
        self.seq = 0
        self.dma = dma
        self.twrites = ()


class Prog:
    def __init__(self):
        self.ops = {e: [] for e in ENGS}
        self.last_w = {}
        self.readers = {}
        self.dma_groups = {}
        self.bar = None
        self.bar_done = set()

    def barrier(self):
        lasts = []
        for e in ENGS:
            for op in reversed(self.ops[e]):
                if op.dma is None:
                    lasts.append(op)
                    break
        seen = set()
        for e in ENGS:
            for op in reversed(self.ops[e]):
                if op.dma is not None and op.dma not in seen:
                    seen.add(op.dma)
                    lasts.append(op)
        self.bar = lasts
        self.bar_done = set()

    def add(self, eng, fn, reads=(), writes=(), xreads=(), dma=None, extra=()):
        op = Op(eng, fn, dma)
        op.twrites = tuple(writes)
        deps = {}

        def consider(d, raw):
            if d is None or d is op:
                return
            if d.eng == eng and d.dma is None and not raw:
                return
            deps[id(d)] = d

        for r in reads:
            consider(self.last_w.get(r), True)
        for r in xreads:
            w = self.last_w.get(r)
            consider(w, w is not None and r in w.twrites)
            for rd in self.readers.get(r, ()):
                consider(rd, False)
        for r in writes:
            consider(self.last_w.get(r), False)
            for rd in self.readers.get(r, ()):
                consider(rd, False)
        for d in extra:
            consider(d, True)
        if self.bar is not None and eng not in self.bar_done:
            self.bar_done.add(eng)
            for d in self.bar:
                consider(d, True)
        op.deps = [(d, self.dma_groups[d.dma]["count"] if d.dma is not None else 0) for d in deps.values()]
        for d, _ in op.deps:
            d.signal = True
        for r in reads:
            self.readers.setdefault(r, []).append(op)
        for r in list(writes) + list(xreads):
            self.last_w[r] = op
            self.readers[r] = []
        if dma is not None:
            g = self.dma_groups.setdefault(dma, {"eng": eng, "count": 0})
            assert g["eng"] == eng, (dma, g["eng"], eng)
            g["count"] += 1
        self.ops[eng].append(op)
        return op

    def emit(self, nc, final_dma_groups):
        with ExitStack() as st:
            esem = {e: st.enter_context(nc.semaphore("s_" + e)) for e in ENGS}
            dsem = {g: st.enter_context(nc.semaphore("d_" + g)) for g in self.dma_groups}
            for e in ENGS:
                c = 0
                for op in self.ops[e]:
                    if op.dma is None and op.signal:
                        c += 1
                        op.seq = c
            block = st.enter_context(nc.Block())

            def run(e, eng):
                known = {}
                for op in self.ops[e]:
                    need = {}
                    for d, cnt in op.deps:
                        if d.dma is not None:
                            k = ("d", d.dma)
                            v = 16 * cnt
                        else:
                            k = ("e", d.eng)
                            v = d.seq
                        if v > need.get(k, 0):
                            need[k] = v
                    for k, v in need.items():
                        if known.get(k, 0) >= v:
                            continue
                        known[k] = v
                        eng.wait_ge(dsem[k[1]] if k[0] == "d" else esem[k[1]], v)
                    ins = op.fn(eng)
                    if op.dma is not None:
                        ins.then_inc(dsem[op.dma], 16)
                    elif op.signal:
                        ins.then_inc(esem[e], 1)
                if e == "sp":
                    for g in final_dma_groups:
                        eng.wait_ge(dsem[g], 16 * self.dma_groups[g]["count"])

            @block.tensor
            def _(eng):
                run("pe", eng)

            @block.scalar
            def _(eng):
                run("act", eng)

            @block.vector
            def _(eng):
                run("dve", eng)

            @block.gpsimd
            def _(eng):
                run("pool", eng)

            @block.sync
            def _(eng):
                run("sp", eng)


def _attn_variant(i):
    def rs(r):
        return min(max(r - 4, 0), 24)
    r0 = 2 * i
    jb = rs(r0) // 2
    p = np.arange(128)
    half = p // 64
    kc = p % 64
    qc = np.arange(64)
    cstart = np.clip(qc - 8, 0, 48)
    vcol = (kc[:, None] >= cstart[None, :]) & (kc[:, None] < cstart[None, :] + 16)
    dc = np.clip(kc[:, None] - qc[None, :], -15, 15) + 15
    drA = np.zeros((128, 4, 2, 64), np.int64)
    mA = np.zeros((128, 4, 2, 64), bool)
    for c in range(4):
        kr = 2 * (jb + c) + half
        for qh in range(2):
            r = r0 + qh
            vrow = (kr >= rs(r)) & (kr <= rs(r) + 7)
            drA[:, c, qh, :] = np.clip(kr - r, -7, 7)[:, None] + 7
            mA[:, c, qh, :] = vrow[:, None] & vcol
    dcA = np.broadcast_to(dc[:, None, None, :], (128, 4, 2, 64))
    extra = rs(r0 + 1) % 2 == 1
    drB = mB = None
    if extra:
        kr = 2 * (jb + 4) + half
        r = r0 + 1
        vrow = (kr >= rs(r)) & (kr <= rs(r) + 7)
        drB = np.broadcast_to(np.clip(kr - r, -7, 7)[:, None] + 7, (128, 64))
        mB = vrow[:, None] & vcol
    return jb, extra, drA, dcA, mA, drB, dc, mB


VARIANT_OF_TILE = {0: 1, 1: 2, 14: 3, 15: 4}


def _host_constants():
    c = {}
    c["ident"] = np.eye(128, dtype=ml_dtypes.bfloat16)
    n = np.arange(128)
    ang = 2.0 * np.pi * ((n[:, None] * n[None, :]) % 128) / 128.0
    c["cc"] = (np.cos(ang) / 512.0).astype(ml_dtypes.bfloat16)
    c["sc"] = (np.sin(ang) / 512.0).astype(ml_dtypes.bfloat16)
    n1 = np.arange(128)
    n2 = np.arange(16)
    k1 = np.arange(128)
    tt = 16 * n1[:, None] + n2[None, :]
    a1 = 2.0 * np.pi * ((tt[:, :, None] * k1[None, None, :]) % S) / S
    c["mc"] = np.cos(a1).astype(ml_dtypes.bfloat16)
    c["ms"] = np.sin(a1).astype(ml_dtypes.bfloat16)
    c["nms"] = (-np.sin(a1)).astype(ml_dtypes.bfloat16)
    a2 = 2.0 * np.pi * np.outer(np.arange(16), np.arange(16)) / 16.0
    eye8 = np.eye(8)
    c["bc"] = np.kron(np.cos(a2), eye8).astype(ml_dtypes.bfloat16)
    c["bsn"] = np.kron(-np.sin(a2), eye8).astype(ml_dtypes.bfloat16)
    c["ones"] = np.ones((128, 128), dtype=ml_dtypes.bfloat16)
    return c


_CONST = None


def _layout_inputs(inp):
    global _CONST
    if _CONST is None:
        _CONST = _host_constants()
    f32 = np.float32

    def bc(v):
        return np.ascontiguousarray(np.broadcast_to(np.asarray(v, f32)[None, :], (128, v.shape[0])))

    def col(v, nch):
        return np.ascontiguousarray(np.asarray(v, f32).reshape(nch, 128).T)

    rpb = np.asarray(inp["rpb"][0], f32)
    biasA = np.zeros((5, 128, 8, 512), f32)
    maskA = np.zeros((5, 128, 512), f32)
    biasB = np.zeros((128, 8, 64), f32)
    maskB = np.zeros((128, 64), f32)
    for i, v in [(5, 0), (0, 1), (1, 2), (14, 3), (15, 4)]:
        jb, extra, drA, dcA, mA, drB, dcB, mB = _attn_variant(i)
        g = rpb[:, drA, dcA]
        biasA[v] = g.reshape(8, 128, 512).transpose(1, 0, 2)
        maskA[v] = mA.reshape(128, 512).astype(f32)
        if v == 0:
            gb = rpb[:, drB, dcB]
            biasB[:] = gb.transpose(1, 0, 2)
            maskB[:] = mB.astype(f32)
    shared = dict(_CONST)
    shared.update({
        "w_ada": np.ascontiguousarray(inp["w_ada"][0], f32),
        "b_ada_bc": bc(inp["b_ada"][0]),
        "g_mix_bc": bc(inp["g_mix"][0]),
        "g_ffn_bc": bc(inp["g_ffn"][0]),
        "g_fin_bc": bc(inp["g_final"]),
        "w_in": np.ascontiguousarray(inp["w_in"][0], f32),
        "w_four": np.ascontiguousarray(inp["w_four"][0], f32),
        "w_out": np.ascontiguousarray(inp["w_out"][0], f32),
        "w_up": np.ascontiguousarray(inp["w_up"][0], f32),
        "w_down": np.ascontiguousarray(inp["w_down"][0], f32),
        "gout_col": np.ascontiguousarray(np.concatenate([col(inp["g_four_out"][0], 4), col(inp["g_na_out"][0], 4)], axis=1)),
        "convw_col": np.ascontiguousarray(np.asarray(inp["conv_w"][0], f32).reshape(3, 44, 128).transpose(2, 1, 0)),
        "convb_col": col(inp["conv_b"][0], 44),
        "biasA": biasA, "maskA": maskA, "biasB": biasB, "maskB": maskB,
    })
    maps = []
    for b in range(8):
        m = dict(shared)
        m["x"] = np.ascontiguousarray(inp["x"][b], f32)
        m["ccol"] = col(inp["c"][b], 8)
        maps.append(m)
    return maps


def build_nc():
    nc = bass.Bass("TRN2", target_bir_lowering=False)

    def din(name, shape, dt=F32):
        return nc.dram_tensor(name, shape, dt, kind="ExternalInput").ap()

    x = din("x", [S, D])
    ccol = din("ccol", [128, 8])
    w_ada = din("w_ada", [D, 6 * D])
    b_ada_bc = din("b_ada_bc", [128, 6 * D])
    g_mix_bc = din("g_mix_bc", [128, D])
    g_ffn_bc = din("g_ffn_bc", [128, D])
    g_fin_bc = din("g_fin_bc", [128, D])
    w_in = din("w_in", [D, 2048])
    w_four = din("w_four", [4, 128, 128])
    w_out = din("w_out", [D, D])
    w_up = din("w_up", [D, 2 * DFF])
    w_down = din("w_down", [DFF, D])
    gout_col_d = din("gout_col", [128, 8])
    convw_d = din("convw_col", [128, 44, 3])
    convb_d = din("convb_col", [128, 44])
    biasA_d = din("biasA", [5, 128, 8, 512])
    maskA_d = din("maskA", [5, 128, 512])
    biasB_d = din("biasB", [128, 8, 64])
    maskB_d = din("maskB", [128, 64])
    ident_d = din("ident", [128, 128], BF16)
    cc_d = din("cc", [128, 128], BF16)
    sc_d = din("sc", [128, 128], BF16)
    mc_d = din("mc", [128, 16, 128], BF16)
    ms_d = din("ms", [128, 16, 128], BF16)
    nms_d = din("nms", [128, 16, 128], BF16)
    bc_d = din("bc", [128, 128], BF16)
    bsn_d = din("bsn", [128, 128], BF16)
    ones_d = din("ones", [128, 128], BF16)
    y_out = nc.dram_tensor("y", [S, D], F32, kind="ExternalOutput").ap()
    dbg = {}

    P = Prog()
    with ExitStack() as st:
        ARENA_F32 = 51712
        arena = st.enter_context(nc.sbuf_tensor("arena", [128, ARENA_F32], F32))
        ps = st.enter_context(nc.psum_tensor("ps", [128, 8, 512], F32))

        def A(off, n_elem, dt):
            assert off % 4 == 0
            nb = n_elem * (2 if dt == BF16 else 4)
            assert nb % 4 == 0 and off + nb <= ARENA_F32 * 4, (off, nb)
            v = arena[:, off // 4:(off + nb) // 4]
            return v.bitcast(BF16) if dt == BF16 else v

        def sb(name, shape, dt):
            return st.enter_context(nc.sbuf_tensor(name, shape, dt))

        K = 1024
        ident = sb("ident_sb", [128, 128], BF16)
        ones = sb("ones_sb", [128, 128], BF16)
        c_sb = sb("c_sb", [128, 8], F32)
        epsb = sb("epsb", [128, 1], F32)
        ss = sb("ss", [128, 5, NT], F32)
        rstd = sb("rstd", [128, 5, NT], F32)
        gout_col = sb("gout_col_sb", [128, 8], F32)
        convw = sb("convw_sb", [128, 44, 3], F32)
        convb = sb("convb_sb", [128, 44], F32)
        rden = sb("rden", [128, 2, 8], F32)

        TOP = 176 * K
        G1 = A(TOP + 0 * K, 1024, F32)
        sh1 = A(TOP + 4 * K, 1024, F32)
        gt1 = A(TOP + 8 * K, 1024, F32)
        G2 = A(TOP + 12 * K, 1024, F32)
        sh2 = A(TOP + 16 * K, 1024, F32)
        gt2 = A(TOP + 22 * K, 1024, F32)

        def dma_sp(out, in_, writes=(), reads=(), grp="c0"):
            return P.add("sp", lambda e: e.dma_start(out=out, in_=in_), reads=reads, writes=writes, dma=grp)

        def dma_pool(out, in_, writes=(), reads=(), grp="p0"):
            return P.add("pool", lambda e: e.dma_start(out=out, in_=in_), reads=reads, writes=writes, dma=grp)

        def mm(out, lhsT, rhs, start, stop, reads, writes):
            return P.add("pe", lambda e: e.matmul(out, lhsT=lhsT, rhs=rhs, start=start, stop=stop), reads=reads, writes=writes)

        def tr(out, in_, reads, writes):
            return P.add("pe", lambda e: e.transpose(out=out, in_=in_, identity=ident[:]), reads=list(reads) + ["ident"], writes=writes)

        def act(out, in_, func, reads=(), writes=(), xreads=(), scale=None, bias=None, accum=None):
            kw = {}
            if scale is not None:
                kw["scale"] = scale
            if bias is not None:
                kw["bias"] = bias
            if accum is not None:
                kw["accum_out"] = accum
            return P.add("act", lambda e: e.activation(out=out, in_=in_, func=func, **kw), reads=reads, writes=writes, xreads=xreads)

        def tt(eng, out, in0, in1, op, reads=(), writes=(), xreads=()):
            return P.add(eng, lambda e: e.tensor_tensor(out=out, in0=in0, in1=in1, op=op), reads=reads, writes=writes, xreads=xreads)

        def stt(eng, out, in0, scalar, in1, op0, op1, reads=(), writes=(), xreads=()):
            return P.add(eng, lambda e: e.scalar_tensor_tensor(out=out, in0=in0, scalar=scalar, in1=in1, op0=op0, op1=op1),
                         reads=reads, writes=writes, xreads=xreads)

        def tsc(eng, out, in0, s1, op0, reads=(), writes=(), xreads=()):
            return P.add(eng, lambda e: e.tensor_scalar(out=out, in0=in0, scalar1=s1, scalar2=None, op0=op0),
                         reads=reads, writes=writes, xreads=xreads)

        def recip(out, in_, reads=(), writes=(), xreads=()):
            return P.add("dve", lambda e: e.reciprocal(out=out, in_=in_), reads=reads, writes=writes, xreads=xreads)

        def cpy(eng, out, in_, reads=(), writes=(), xreads=()):
            return P.add(eng, lambda e: e.tensor_copy(out=out, in_=in_), reads=reads, writes=writes, xreads=xreads)

        def mset(eng, ap, val, writes):
            return P.add(eng, lambda e: e.memset(ap, val), writes=writes)

        def dump(name, ap_sb, shape, dt, reads):
            if name not in DEBUG:
                return
            t = nc.dram_tensor("dbg_" + name, shape, dt, kind="ExternalOutput").ap()
            dbg[name] = t
            P.add("sp", lambda e: e.dma_start(out=t, in_=ap_sb), reads=reads, dma="out")

        def rstd_of(sidx, i, n):
            act(rstd[:, sidx, i:i + 1], ss[:, sidx, i:i + 1], AF.Sqrt, reads=[("ss", sidx, i), "epsb"], writes=[("rstd", sidx, i)],
                scale=1.0 / n, bias=epsb[:, 0:1])
            recip(rstd[:, sidx, i:i + 1], rstd[:, sidx, i:i + 1], reads=[("rstd", sidx, i)], writes=[("rstd", sidx, i)])

        cs_rep = sb("cs_rep", [128, 8, 128], BF16)

        dma_sp(ident[:], ident_d, writes=["ident"])
        dma_sp(c_sb[:], ccol, writes=["c_sb"])
        dma_sp(gout_col[:], gout_col_d, writes=["gout_col"])
        dma_sp(convw[:], convw_d, writes=["convw"])
        dma_sp(convb[:], convb_d, writes=["convb"])
        mset("dve", epsb[:], EPS, ["epsb"])
        act(cs_rep[:], c_sb[:].unsqueeze(2).broadcast_to([128, 8, 128]), AF.Silu, reads=["c_sb"], writes=["cs_rep"])
        mpieces = [(1, G1, g_mix_bc), (0, sh1, None), (2, gt1, None), (4, G2, g_ffn_bc), (3, sh2, None), (5, gt2, None)]
        mcast = [0]

        def mod_piece(pi, w32, wbf, bada_b, gtmp_b, tag, banks, ncols=1024, dq=None):
            blk, dst, gsrc = mpieces[pi]
            dq = dq or dma_sp
            dq(bada_b, b_ada_bc[:, blk * 1024:(blk + 1) * 1024], writes=[("bada", tag)], grp=f"bada{tag}")
            if gsrc is not None:
                dq(gtmp_b, gsrc, writes=[("gtmp", tag)], grp=f"bada{tag}")
            for rnd in range(1024 // ncols):
                c0 = blk * 1024 + rnd * ncols
                for kc in range(8):
                    dq(w32[:, kc, :], w_ada[kc * 128:(kc + 1) * 128, c0:c0 + ncols],
                       writes=[("w32", tag, kc)], grp=f"w32{tag}_{kc}")
                for kc in range(8):
                    eng = "dve" if mcast[0] % 3 != 2 else "act"
                    mcast[0] += 1
                    if eng == "dve":
                        cpy("dve", wbf[:, kc, :], w32[:, kc, :], reads=[("w32", tag, kc)], writes=[("wada", tag, kc)])
                    else:
                        act(wbf[:, kc, :], w32[:, kc, :], AF.Copy, reads=[("w32", tag, kc)], writes=[("wada", tag, kc)])
                for h2 in range(ncols // 512):
                    hh = rnd * (ncols // 512) + h2
                    bank = banks[hh]
                    hs = slice(hh * 512, (hh + 1) * 512)
                    for kc in range(8):
                        mm(ps[:, bank, :], cs_rep[:, kc, :], wbf[:, kc, h2 * 512:(h2 + 1) * 512], kc == 0, kc == 7,
                           reads=["cs_rep", ("wada", tag, kc)], writes=[("ps", bank)])
                    dname = ("mod", pi, hh)
                    tt("dve", dst[:, hs], ps[:, bank, :], bada_b[:, hs], ALU.add,
                       reads=[("bada", tag)], xreads=[("ps", bank)], writes=[dname])
                    if gsrc is not None:
                        stt("dve", dst[:, hs], dst[:, hs], 1.0, gtmp_b[:, hs], ALU.add, ALU.mult,
                            reads=[dname, ("gtmp", tag)], writes=[dname])

        w32a = [A(64 * K + i * 32 * K, 8 * 1024, F32).rearrange("p (c n) -> p c n", c=8) for i in range(2)]
        wbfa = [A(128 * K + i * 16 * K, 8 * 1024, BF16).rearrange("p (c n) -> p c n", c=8) for i in range(2)]
        badaa = [A(160 * K + i * 4 * K, 1024, F32) for i in range(2)]
        gtmpa = [A(168 * K + i * 4 * K, 1024, F32) for i in range(2)]
        for pi in range(2):
            b = pi % 2
            mod_piece(pi, w32a[b], wbfa[b], badaa[b], gtmpa[b], b, (2 * b, 2 * b + 1))
        win = A(32 * K, 8 * 2048, BF16).rearrange("p (c n) -> p c n", c=8)
        wi32 = [A(i * 8 * K, 2048, F32) for i in range(4)]
        for kc in range(8):
            b4 = kc % 4
            dma_sp(wi32[b4], w_in[kc * 128:(kc + 1) * 128, :], writes=[("wi32", b4)], grp=f"wi32{b4}")
            cpy("dve", win[:, kc, :], wi32[b4], reads=[("wi32", b4)], writes=[("win", kc)])
        MOD = {name: [("mod", pi, 0), ("mod", pi, 1)] for pi, name in enumerate(["G1", "sh1", "gt1", "G2", "sh2", "gt2"])}
        P.barrier()

        hT = A(0, 8 * S, BF16).rearrange("p (c t) -> p c t", c=8)
        qT = A(64 * K, 4 * S, BF16).rearrange("p (c t) -> p c t", c=4)
        kT = A(80 * K, 4 * S, BF16).rearrange("p (c t) -> p c t", c=4)
        VO = 96 * K
        Vext = A(VO, NT * 8 * 65, BF16).rearrange("p (t h d) -> p t h d", t=NT, h=8)
        UO = VO + 16640 + 256
        uT = A(UO, 4 * S, BF16).rearrange("p (c t) -> p c t", c=4)
        XO = UO + 16 * K
        xt = [A(XO + i * 4 * K, 1024, F32) for i in range(3)]
        t1_a = [A(XO + 12 * K + i * 4 * K, 1024, F32) for i in range(2)]
        hb_a = [A(XO + 20 * K + i * 2 * K, 1024, BF16) for i in range(2)]
        junk_a = A(XO + 24 * K, 1024, BF16)
        assert XO + 26 * K <= TOP

        mset("dve", Vext[:, :, :, 64:65], 1.0, ["vones"])

        def norm_s1(i, src_ap, src_res, sidx, Gt, Gres, sht, shres, t1, hb, junk, pool_ok=True):
            b2 = i % 2
            act(junk, src_ap, AF.Square, reads=src_res, writes=["junk", ("ss", sidx, i)], accum=ss[:, sidx, i:i + 1])
            rstd_of(sidx, i, D)
            stt("dve", t1[b2], src_ap, rstd[:, sidx, i:i + 1], Gt, ALU.mult, ALU.mult,
                reads=list(src_res) + [("rstd", sidx, i)] + Gres, writes=[("t1", b2)])
            tt("pool" if (pool_ok and i % 2 == 0) else "dve", hb[b2], t1[b2], sht, ALU.add,
               reads=[("t1", b2)] + shres, writes=[("hb", b2)])

        def norm_s2(i, dstT, dst_res, col0, hb, tbanks):
            b2 = i % 2
            bank = tbanks[b2]
            pst = ps[:, bank, :].bitcast(BF16)
            for c in range(8):
                tr(pst[:, c * 128:(c + 1) * 128], hb[b2][:, c * 128:(c + 1) * 128], reads=[("hb", b2)], writes=[("ps", bank)])
            act(dstT[:, :, col0 + i * 128:col0 + (i + 1) * 128], pst.rearrange("p (c t) -> p c t", c=8), AF.Copy,
                xreads=[("ps", bank)], writes=[(dst_res, i)])

        pbank = [0]

        def next_bank():
            b = pbank[0]
            pbank[0] = b + 1 if b < 4 else 0
            return b

        def proj_groups(tb):
            ts_ = slice(tb * 512, (tb + 1) * 512)
            hres = [("hT", 4 * tb + j) for j in range(4)]
            out = []

            def fm(col_base, dst, dst_name, scale, fc):
                bank = next_bank()
                for kc in range(8):
                    mm(ps[:, bank, :], win[:, kc, col_base + fc * 128:col_base + (fc + 1) * 128], hT[:, kc, ts_], kc == 0, kc == 7,
                       reads=[("win", kc)] + hres, writes=[("ps", bank)])
                if scale is None:
                    cpy("dve", dst[:, fc, ts_], ps[:, bank, :], xreads=[("ps", bank)], writes=[(dst_name, fc, tb)])
                else:
                    act(dst[:, fc, ts_], ps[:, bank, :], AF.Copy, xreads=[("ps", bank)], writes=[(dst_name, fc, tb)], scale=scale)

            def vm(tt_):
                bank = next_bank()
                for kc in range(8):
                    mm(ps[:, bank, :], hT[:, kc, tt_ * 128:(tt_ + 1) * 128], win[:, kc, 1536:2048], kc == 0, kc == 7,
                       reads=[("win", kc), ("hT", tt_)], writes=[("ps", bank)])
                act(Vext[:, tt_, :, 0:64], ps[:, bank, :].rearrange("p (h d) -> p h d", h=8), AF.Copy,
                    xreads=[("ps", bank)], writes=[("V", tt_)])

            for col_base, dst, dst_name, scale in ((0, uT, "uT", None), (512, qT, "qT", 0.125), (1024, kT, "kT", None)):
                for fc in range(4):
                    out.append(lambda cb=col_base, d=dst, dn=dst_name, sc=scale, fc=fc: fm(cb, d, dn, sc, fc))
            for tt_ in range(4 * tb, 4 * tb + 4):
                out.append(lambda tt_=tt_: vm(tt_))
            return out

        def p1_s1(i):
            if i >= NT:
                return
            b3 = i % 3
            dma_sp(xt[b3], x[i * 128:(i + 1) * 128, :], writes=[("xt", b3)], grp=f"xt{b3}")
            norm_s1(i, xt[b3], [("xt", b3)], 0, G1, MOD["G1"], sh1, MOD["sh1"], t1_a, hb_a, junk_a, pool_ok=False)

        LS0 = XO + 26 * K
        lwb = [A(LS0 + b * 8 * K, 8 * 512, BF16).rearrange("p (c n) -> p c n", c=8) for b in range(2)]
        lgt = A(LS0 + 16 * K, 1024, F32)
        assert LS0 + 20 * K <= TOP
        LGT = ["lgt", "lgt"]
        lrounds = [(pi, r) for pi in range(2, 6) for r in range(2)]

        def late_dma(k):
            pi, r = lrounds[k]
            blk, dst, gsrc = mpieces[pi]
            if r == 0:
                dma_sp(dst, b_ada_bc[:, blk * 1024:(blk + 1) * 1024], writes=[("mod", pi, 0), ("mod", pi, 1)], grp=f"lb{pi}")
                if gsrc is not None:
                    dma_sp(lgt, gsrc, writes=["lgt"], grp=f"lb{pi}")
            c0 = blk * 1024 + r * 512
            for kc in range(8):
                dma_pool(lwb[k % 2][:, kc, :], w_ada[kc * 128:(kc + 1) * 128, c0:c0 + 512],
                         writes=[("lwb", k % 2, kc)], grp=f"lw{k % 2}_{kc}")

        def late_round(k):
            pi, r = lrounds[k]
            blk, dst, gsrc = mpieces[pi]
            hs = slice(r * 512, (r + 1) * 512)
            for kc in range(8):
                mm(ps[:, 5, :], cs_rep[:, kc, :], lwb[k % 2][:, kc, :], kc == 0, kc == 7,
                   reads=["cs_rep", ("lwb", k % 2, kc)], writes=[("ps", 5)])
            tt("dve", dst[:, hs], ps[:, 5, :], dst[:, hs], ALU.add, reads=[("mod", pi, r)], xreads=[("ps", 5)], writes=[("mod", pi, r)])
            if gsrc is not None:
                stt("dve", dst[:, hs], dst[:, hs], 1.0, lgt[:, hs], ALU.add, ALU.mult, reads=[("mod", pi, r), LGT[r]], writes=[("mod", pi, r)])

        def late_hook(j):
            if j % 2 == 1 and j >= 3:
                late_round((j - 3) // 2)
            if j % 2 == 0:
                late_dma(j // 2)

        p1_s1(0)
        p1_s1(1)
        for i in range(4):
            norm_s2(i, hT, "hT", 0, hb_a, (6, 7))
            p1_s1(i + 2)
            late_hook(i)
        for tb in range(4):
            groups = proj_groups(tb)
            for gi, g in enumerate(groups):
                g()
                if gi % 4 == 3 and tb < 3:
                    i = 4 * (tb + 1) + gi // 4
                    norm_s2(i, hT, "hT", 0, hb_a, (6, 7))
                    p1_s1(i + 2)
                    late_hook(i)
        late_round(7)
        dump("hT", hT, [128, 8, S], BF16, [("hT", i) for i in range(NT)])
        dump("qT", qT, [128, 4, S], BF16, [("qT", fc, tb) for fc in range(4) for tb in range(4)])
        dump("kT", kT, [128, 4, S], BF16, [("kT", fc, tb) for fc in range(4) for tb in range(4)])
        dump("uT", uT, [128, 4, S], BF16, [("uT", fc, tb) for fc in range(4) for tb in range(4)])
        dump("V", Vext, [128, NT, 8, 65], BF16, [("V", t) for t in range(NT)] + ["vones"])
        P.barrier()

        EA = A(0, 5 * 8 * 512, BF16).rearrange("p (v h n) -> p v h n", v=5, h=8)
        EB = A(40 * K, 8 * 64, BF16).rearrange("p (h n) -> p h n", h=8)
        bstage = A(41 * K, 8 * 512, F32).rearrange("p (h n) -> p h n", h=8)
        mstage = A(57 * K, 512, F32)
        Y0 = UO + 16 * K
        ynaT = A(Y0, 4 * S, BF16).rearrange("p (c t) -> p c t", c=4)
        yfT = A(Y0 + 16 * K, 4 * S, BF16).rearrange("p (c t) -> p c t", c=4)
        o_n = [A(Y0 + 32 * K + i * 2 * K, 512, F32) for i in range(2)]
        o_bf = [A(Y0 + 36 * K + i * K, 512, BF16) for i in range(2)]
        junk_b = A(Y0 + 38 * K, 512, BF16)
        PB0 = Y0 + 20 * K
        peA = [A(PB0 + i * 2 * K, 1024, BF16).rearrange("p (e n) -> p e n", e=2) for i in range(2)]
        pTA = [A(PB0 + 4 * K + i * 2 * K, 1024, BF16).rearrange("p (e n) -> p e n", e=2) for i in range(2)]
        peB = [A(PB0 + 8 * K + i * 256, 128, BF16).rearrange("p (e n) -> p e n", e=2) for i in range(2)]
        pTB = [A(PB0 + 8 * K + 512 + i * 256, 128, BF16).rearrange("p (e n) -> p e n", e=2) for i in range(2)]
        bstB = A(Y0 + 16 * K, 8 * 64, F32).rearrange("p (h n) -> p h n", h=8)
        mstB = A(Y0 + 18 * K, 64, F32)
        assert Y0 + 39 * K <= TOP

        bsth = [bstage[:, 0:4, :], bstage[:, 4:8, :]]
        ecount = [0]

        def load_E(v):
            dma_sp(mstage, maskA_d[v], writes=["mstage"], grp="mst")
            for hh in range(2):
                dma_sp(bsth[hh], biasA_d[v][:, hh * 4:(hh + 1) * 4, :],
                       writes=[("bstage", hh), ("bsq", 2 * hh), ("bsq", 2 * hh + 1)], grp=f"bst{hh}")

        def build_E0_quarters():
            dma_sp(mstage, maskA_d[0], writes=["mstage"], grp="mst")
            for q in range(4):
                dma_sp(bstage[:, 2 * q:2 * q + 2, :], biasA_d[0][:, 2 * q:2 * q + 2, :], writes=[("bsq", q)], grp=f"bsq{q}")
            dma_sp(bstB, biasB_d, writes=["bstB"], grp="mstB")
            dma_sp(mstB, maskB_d, writes=["mstB"], grp="mstB")
            for q in range(4):
                act(bstage[:, 2 * q:2 * q + 2, :], bstage[:, 2 * q:2 * q + 2, :], AF.Exp, reads=[("bsq", q)], writes=[("bsq", q)])
                tt("dve", EA[:, 0, 2 * q:2 * q + 2, :], bstage[:, 2 * q:2 * q + 2, :], mstage.unsqueeze(1).broadcast_to([128, 2, 512]),
                   ALU.mult, reads=[("bsq", q), "mstage"], writes=[("EA", 0, q)])
                if q == 0:
                    act(bstB, bstB, AF.Exp, reads=["bstB"], writes=["bstB"])
                    tt("dve", EB, bstB, mstB.unsqueeze(1).broadcast_to([128, 8, 64]), ALU.mult,
                       reads=["bstB", "mstB"], writes=["EB"])

        def build_E_quarter(v, q):
            hh = q // 2
            sl = bstage[:, 2 * q:2 * q + 2, :]
            act(sl, sl, AF.Exp, reads=[("bstage", hh)], writes=[("bstage", hh)])
            tt("dve", EA[:, v, 2 * q:2 * q + 2, :], sl, mstage.unsqueeze(1).broadcast_to([128, 2, 512]), ALU.mult,
               reads=[("bstage", hh), "mstage"], writes=[("EA", v, q)])

        def build_E(v, preloaded=False):
            if not preloaded:
                dma_sp(mstage, maskA_d[v], writes=["mstage"], grp="mst")
            for hh in range(2):
                bsl = hh
                if not preloaded:
                    dma_sp(bsth[bsl], biasA_d[v][:, hh * 4:(hh + 1) * 4, :], writes=[("bstage", bsl)], grp=f"bst{bsl}")
                act(bsth[bsl], bsth[bsl], AF.Exp, reads=[("bstage", bsl)], writes=[("bstage", bsl)])
                tt("dve", EA[:, v, hh * 4:(hh + 1) * 4, :], bsth[bsl], mstage.unsqueeze(1).broadcast_to([128, 4, 512]), ALU.mult,
                   reads=[("bstage", bsl), "mstage"], writes=[("EA", v, 2 * hh), ("EA", v, 2 * hh + 1)])
            if v == 0:
                dma_sp(bstB, biasB_d, writes=["bstB"], grp="mstB")
                dma_sp(mstB, maskB_d, writes=["mstB"], grp="mstB")
                act(bstB, bstB, AF.Exp, reads=["bstB"], writes=["bstB"])
                tt("dve", EB, bstB, mstB.unsqueeze(1).broadcast_to([128, 8, 64]), ALU.mult,
                   reads=["bstB", "mstB"], writes=["EB"])

        wf = A(Y0 + 18 * K + 512, 4 * 128, BF16).rearrange("p (g n) -> p g n", g=4)
        ccs = A(Y0 + 19 * K + 512, 256, BF16)
        AB = cs_rep[:].rearrange("p c m -> p (c m)").rearrange("p (g n) -> p g n", g=4)
        def load_AB():
            dma_sp(ccs[:, 0:128], cc_d, writes=["ccs"])
            dma_sp(ccs[:, 128:256], sc_d, writes=["ccs"])
            for g in range(4):
                dma_pool(wf[:, g, :], w_four[g], writes=[("wf", g)], grp=f"wf{g}")

        def build_AB():
            for g in range(4):
                for s2 in range(2):
                    mm(ps[:, 5, s2 * 128:(s2 + 1) * 128], ccs[:, s2 * 128:(s2 + 1) * 128], wf[:, g, :],
                       True, True, reads=["ccs", ("wf", g)], writes=[("ps", 5)])
                act(AB[:, g, :], ps[:, 5, 0:256], AF.Copy, xreads=[("ps", 5)], writes=["AB"])

        build_E0_quarters()

        TORDER = list(range(2, 14)) + [0, 1, 14, 15]
        useq = [(i, hp) for i in TORDER for hp in range(4)]
        tinfo = {i: (_attn_variant(i)[0], _attn_variant(i)[1], VARIANT_OF_TILE.get(i, 0)) for i in range(NT)}
        po = [ps[:, 6 + g, 0:260].rearrange("p (h d) -> p h d", h=4) for g in range(2)]

        def attn_qk(m):
            i, hp = useq[m]
            jb, extra, v = tinfo[i]
            u2 = m % 2
            qs = slice(i * 128, (i + 1) * 128)
            def qk_a(c, e):
                pb = e * 64
                mm(ps[:, u2 + 2 * e, c * 128:(c + 1) * 128], kT[pb:pb + 64, hp, (jb + c) * 128:(jb + c + 1) * 128],
                   qT[pb:pb + 64, hp, qs], True, True,
                   reads=[("kT", hp, (jb + c) // 4), ("qT", hp, i // 4)], writes=[("ps", u2 + 2 * e)])

            def qk_b(e):
                pb = e * 64
                mm(ps[:, 4 + u2, 256 + e * 64:256 + (e + 1) * 64], kT[pb:pb + 64, hp, (jb + 4) * 128:(jb + 5) * 128],
                   qT[pb:pb + 64, hp, i * 128 + 64:(i + 1) * 128], True, True,
                   reads=[("kT", hp, (jb + 4) // 4), ("qT", hp, i // 4)], writes=[("ps", 4 + u2)])

            for e in range(2):
                for c in range(4):
                    qk_a(c, e)
                if extra:
                    qk_b(e)

        def attn_soft(m):
            i, hp = useq[m]
            jb, extra, v = tinfo[i]
            u2 = m % 2
            for e in range(2):
                act(peA[u2][:, e, :], ps[:, u2 + 2 * e, :], AF.Exp, xreads=[("ps", u2 + 2 * e)], writes=[("peA", u2, e)])
                if e == 0 and extra:
                    act(peB[u2], ps[:, 4 + u2, 256:384].rearrange("p (e n) -> p e n", e=2), AF.Exp,
                        xreads=[("ps", 4 + u2)], writes=[("peB", u2)])
            for e in range(2):
                tt("dve", pTA[u2][:, e, :], peA[u2][:, e, :], EA[:, v, 2 * hp + e, :], ALU.mult,
                   reads=[("peA", u2, e), ("EA", v, hp)], writes=[("pTA", u2, e)])
                if e == 0 and extra:
                    tt("dve", pTB[u2], peB[u2], EB[:, 2 * hp:2 * hp + 2, :], ALU.mult, reads=[("peB", u2), "EB"], writes=[("pTB", u2)])

        def attn_pv(m):
            i, hp = useq[m]
            jb, extra, v = tinfo[i]
            u2 = m % 2
            nch = 5 if extra else 4
            for e in range(2):
                h = 2 * hp + e
                pv = po[h // 4]
                hb4 = h % 4
                for c in range(nch):
                    for qh in range(2):
                        if c == 4 and qh == 0:
                            continue
                        last = (c == 3 and not (extra and qh == 1)) or c == 4
                        if c < 4:
                            lhs = pTA[u2][:, e, c * 128 + qh * 64:c * 128 + (qh + 1) * 64]
                            rd = [("pTA", u2, e)]
                        else:
                            lhs = pTB[u2][:, e, :]
                            rd = [("pTB", u2)]
                        mm(pv[qh * 64:(qh + 1) * 64, hb4, :], lhs, Vext[:, jb + c, h, :], c == 0, last,
                           reads=rd + [("V", jb + c), "vones"], writes=[("ps", 6 + h // 4)])

        def attn_out_a(i):
            b2 = i % 2
            for g in range(2):
                recip(rden[:, b2, g * 4:(g + 1) * 4], po[g][:, :, 64], xreads=[("ps", 6 + g)], writes=[("rden", b2, g)])
                tt("dve", o_n[b2][:, g * 256:(g + 1) * 256].rearrange("p (h d) -> p h d", h=4), po[g][:, :, 0:64],
                   rden[:, b2, g * 4:(g + 1) * 4].unsqueeze(2).broadcast_to([128, 4, 64]), ALU.mult,
                   reads=[("rden", b2, g)], xreads=[("ps", 6 + g)], writes=[("o_n", b2, g)])

        def attn_out_a2(i):
            b2 = i % 2
            act(junk_b, o_n[b2], AF.Square, reads=[("o_n", b2, 0), ("o_n", b2, 1)], writes=["junk", ("ss", 1, i)],
                accum=ss[:, 1, i:i + 1])
            act(rstd[:, 1, i:i + 1], ss[:, 1, i:i + 1], AF.Ln, reads=[("ss", 1, i), "epsb"], writes=[("rstd", 1, i)],
                scale=1.0 / 512, bias=epsb[:, 0:1])
            act(rstd[:, 1, i:i + 1], rstd[:, 1, i:i + 1], AF.Exp, reads=[("rstd", 1, i)], writes=[("rstd", 1, i)], scale=-0.5)

        def attn_out_b(i):
            b2 = i % 2
            tsc("dve", o_bf[b2], o_n[b2], rstd[:, 1, i:i + 1], ALU.mult,
                reads=[("o_n", b2, 0), ("o_n", b2, 1), ("rstd", 1, i)], writes=[("o_bf", b2)])
            pst = ps[:, 4, :].bitcast(BF16)
            for c in range(4):
                tr(pst[:, c * 128:(c + 1) * 128], o_bf[b2][:, c * 128:(c + 1) * 128], reads=[("o_bf", b2)], writes=[("ps", 4)])

        def attn_out_c(i):
            pst = ps[:, 4, :].bitcast(BF16)
            cpy("dve", ynaT[:, :, i * 128:(i + 1) * 128], pst[:, 0:512].rearrange("p (c t) -> p c t", c=4),
                xreads=[("ps", 4)], writes=[("ynaT", i)])

        E_SCHED = {(4, 0): 1, (6, 0): 2, (8, 0): 3, (10, 0): 4}
        E_LOAD = {(3, 0): 1, (5, 0): 2, (7, 0): 3, (9, 0): 4}
        attn_qk(0)
        prev = None
        pend_a = None
        for m in range(len(useq)):
            if m + 1 < len(useq):
                attn_qk(m + 1)
            attn_soft(m)
            if pend_a is not None:
                attn_out_a(pend_a)
                pend_a = None
            attn_pv(m)
            i, hp = useq[m]
            if hp == 3:
                pend_a = i
            if prev is not None and hp == 0:
                attn_out_a2(prev)
            if prev is not None and hp == 1:
                attn_out_b(prev)
            if prev is not None and hp == 2:
                attn_out_c(prev)
            if hp == 3:
                prev = i
            if (i, hp) in E_LOAD:
                load_E(E_LOAD[(i, hp)])
            if (i, 0) in E_SCHED:
                build_E_quarter(E_SCHED[(i, 0)], hp)
            if m == 1:
                load_AB()
            if m == 8:
                build_AB()
        attn_out_a(pend_a)
        attn_out_a2(prev)

        PQ = A(64 * K, NT * 2 * 4 * 128, BF16).rearrange("p (t s g n) -> p t s g n", t=NT, s=2, g=4)
        Yst = A(0, 2 * 64 * NT * 8, BF16).rearrange("p (s j t e) -> p s j t e", s=2, j=64, t=NT)
        YT = A(32 * K, 64 * 2 * 128, BF16).rearrange("p (j s n) -> p j s n", j=64, s=2)
        yf = A(96 * K, NT * 512, BF16).rearrange("p (t n) -> p t n", t=NT)
        F0 = Y0 + 32 * K
        mct = A(F0 + 2 * K, 16 * 128, BF16).rearrange("p (t n) -> p t n", t=16)
        mst = A(F0 + 6 * K, 16 * 128, BF16).rearrange("p (t n) -> p t n", t=16)
        nmst = A(F0 + 10 * K, 16 * 128, BF16).rearrange("p (t n) -> p t n", t=16)
        bct = A(F0 + 14 * K, 128, BF16)
        bsnt = A(F0 + 14 * K + 256, 128, BF16)
        assert F0 + 14 * K + 512 <= TOP
        sqf = A(0, 512, F32)
        ybf = [A(2 * K + i * K, 512, BF16) for i in range(2)]
        sqf2 = [A(4 * K + i * 2 * K, 512, F32) for i in range(2)]


        def f_stage0(n2):
            for gp in range(2):
                bank = gp
                for gg in range(2):
                    g = 2 * gp + gg
                    mm(ps[:, bank, gg * 256:(gg + 1) * 256], uT[:, g, n2:S:16], AB[:, g, :], True, True,
                       reads=[("uT", g, tb) for tb in range(4)] + ["AB"], writes=[("ps", bank)])
                src = ps[:, bank, :].rearrange("p (g s n) -> p s g n", g=2, s=2)
                if gp == 0:
                    act(PQ[:, n2, :, 0:2, :], src, AF.Copy, xreads=[("ps", bank)], writes=[("PQ", n2, 0)])
                else:
                    cpy("dve", PQ[:, n2, :, 2:4, :], src, xreads=[("ps", bank)], writes=[("PQ", n2, 1)])

        def f_stage1(n2):
            pr = [("PQ", n2, 0), ("PQ", n2, 1), ("ftab", 0), ("ftab", 1), ("ftab", 2)]
            br = 2 + 2 * (n2 % 2)
            Pall = PQ[:, n2, 0, :, :]
            Qall = PQ[:, n2, 1, :, :]
            mm(ps[:, br, :], mct[:, n2, :], Pall, True, False, reads=pr, writes=[("ps", br)])
            mm(ps[:, br, :], nmst[:, n2, :], Qall, False, True, reads=pr, writes=[("ps", br)])
            mm(ps[:, br + 1, :], mst[:, n2, :], Pall, True, False, reads=pr, writes=[("ps", br + 1)])
            mm(ps[:, br + 1, :], mct[:, n2, :], Qall, False, True, reads=pr, writes=[("ps", br + 1)])
            act(Yst[:, 0, :, n2, :], ps[:, br, :].rearrange("p (j e) -> p j e", e=8), AF.Copy,
                xreads=[("ps", br)], writes=[("Y", n2, 0)])
            cpy("dve", Yst[:, 1, :, n2, :], ps[:, br + 1, :].rearrange("p (j e) -> p j e", e=8),
                xreads=[("ps", br + 1)], writes=[("Y", n2, 1)])

        f_stage0(0)
        f_stage0(1)
        attn_out_b(prev)
        f_stage0(2)
        f_stage0(3)
        attn_out_c(prev)
        dump("ynaT", ynaT, [128, 4, S], BF16, [("ynaT", i) for i in range(NT)])
        att_last = [P.ops[e][-1] for e in ("pe", "act", "dve")]
        for k_, (dst_, src_) in enumerate(((mct, mc_d), (mst, ms_d), (nmst, nms_d), (bct, bc_d), (bsnt, bsn_d))):
            P.add("sp", (lambda o, i_: (lambda e: e.dma_start(out=o, in_=i_)))(dst_, src_), writes=[("ftab", k_)], dma=f"ftab{k_}",
                  extra=att_last)
        for n2 in range(NT):
            if n2 + 4 < NT:
                f_stage0(n2 + 4)
            f_stage1(n2)

        YALL = [("Y", n2, s2) for n2 in range(NT) for s2 in range(2)]

        def f_transp(jb):
            bank = 6 + jb % 2
            pst = ps[:, bank, :].bitcast(BF16)
            for jj in range(4):
                j = 4 * jb + jj
                for s2 in range(2):
                    o = (jj * 2 + s2) * 128
                    tr(pst[:, o:o + 128], Yst[:, s2, j, :, :].rearrange("p t e -> p (t e)"), reads=YALL, writes=[("ps", bank)])
            src = pst.rearrange("p (j s n) -> p j s n", j=4, s=2)
            if jb % 2 == 0:
                act(YT[:, 4 * jb:4 * jb + 4, :, :], src, AF.Copy, xreads=[("ps", bank)], writes=[("YT", jb)])
            else:
                cpy("dve", YT[:, 4 * jb:4 * jb + 4, :, :], src, xreads=[("ps", bank)], writes=[("YT", jb)])

        def f_stage2(jb):
            bank = jb % 2
            for jj in range(4):
                j = 4 * jb + jj
                o = jj * 128
                mm(ps[:, bank, o:o + 128], YT[:, j, 0, :], bct, True, False, reads=[("YT", jb), ("ftab", 3), ("ftab", 4)], writes=[("ps", bank)])
                mm(ps[:, bank, o:o + 128], YT[:, j, 1, :], bsnt, False, True, reads=[("YT", jb), ("ftab", 3), ("ftab", 4)], writes=[("ps", bank)])
            src = ps[:, bank, :].rearrange("p (j k e) -> p k j e", j=4, k=16)
            dst = yf[:, :, 32 * jb:32 * jb + 32].rearrange("p k (j e) -> p k j e", j=4)
            if jb % 2 == 0:
                cpy("dve", dst, src, xreads=[("ps", bank)], writes=[("yf", jb)])
            else:
                act(dst, src, AF.Copy, xreads=[("ps", bank)], writes=[("yf", jb)])

        f_transp(0)
        for jb in range(16):
            if jb + 1 < 16:
                f_transp(jb + 1)
            f_stage2(jb)

        YFALL = [("yf", jb) for jb in range(16)]

        def f_norm_a(k2):
            if k2 >= NT:
                return
            if k2 % 2 == 0:
                act(sqf2[0], yf[:, k2, :], AF.Square, reads=YFALL, writes=[("sqf", 0), ("ss", 4, k2)], accum=ss[:, 4, k2:k2 + 1])
            else:
                tt("pool", sqf2[1], yf[:, k2, :], yf[:, k2, :], ALU.mult, reads=YFALL, writes=[("sqf", 1)])
                P.add("dve", lambda e: e.reduce_sum(out=ss[:, 4, k2:k2 + 1], in_=sqf2[1], axis=mybir.AxisListType.X),
                      reads=[("sqf", 1)], writes=[("ss", 4, k2)])
            rstd_of(4, k2, 512)

        def f_norm_b(k2):
            b2 = k2 % 2
            tsc("dve", ybf[b2], yf[:, k2, :], rstd[:, 4, k2:k2 + 1], ALU.mult, reads=YFALL + [("rstd", 4, k2)], writes=[("ybf", b2)])
            bank = 2 + b2
            pst = ps[:, bank, :].bitcast(BF16)
            for c in range(4):
                tr(pst[:, c * 128:(c + 1) * 128], ybf[b2][:, c * 128:(c + 1) * 128], reads=[("ybf", b2)], writes=[("ps", bank)])
            act(yfT[:, :, k2 * 128:(k2 + 1) * 128], pst[:, 0:512].rearrange("p (c t) -> p c t", c=4), AF.Copy,
                xreads=[("ps", bank)], writes=[("yfT", k2)])

        SP2 = S + 2
        N0 = ((8 * SP2 * 2 + 255) // 256) * 256
        wo = A(N0 + 6 * K, 8 * 1024, BF16).rearrange("p (c n) -> p c n", c=8)
        wst = [A(Y0 + 32 * K + i * 4 * K, 1024, F32) for i in range(2)]
        f_last = [P.ops[e][-1] for e in ("pe", "act", "dve")]

        def wo_dma(ec):
            if ec >= 8:
                return
            b2 = ec % 2
            P.add("sp", (lambda o, i_: (lambda e: e.dma_start(out=o, in_=i_)))(wst[b2], w_out[ec * 128:(ec + 1) * 128, :]),
                  writes=[("wst", b2)], dma=f"wst{b2}", extra=f_last)

        def wo_scale(ec):
            b2 = ec % 2
            P.add("dve", (lambda o, i0, sc: (lambda e: e.scalar_tensor_tensor(out=o, in0=i0, scalar=sc, in1=gt1, op0=ALU.mult, op1=ALU.mult)))(
                wo[:, ec, :], wst[b2], gout_col[:, ec:ec + 1]),
                reads=[("wst", b2), "gout_col"] + MOD["gt1"], writes=[("wo", ec)], extra=f_last)

        wo_dma(0)
        wo_dma(1)
        f_norm_a(0)
        f_norm_a(1)
        for k2 in range(NT):
            f_norm_a(k2 + 2)
            f_norm_b(k2)
            if 1 <= k2 < 9:
                wo_scale(k2 - 1)
                wo_dma(k2 + 1)
        dump("yfT", yfT, [128, 4, S], BF16, [("yfT", k2) for k2 in range(NT)])
        P.barrier()

        hT2 = A(0, 8 * SP2, BF16).rearrange("p (c t) -> p c t", c=8)
        xt4 = [A(N0 + 22 * K + i * 4 * K, 1024, F32) for i in range(2)]
        assert N0 + 30 * K <= 64 * K
        x2 = A(64 * K, NT * 1024, F32).rearrange("p (t n) -> p t n", t=NT)
        t1_c = [A(TOP + i * 4 * K, 1024, F32) for i in range(2)]
        hb_c = [A(Y0 + 40 * K + i * 2 * K, 1024, BF16) for i in range(2)]
        junk_c = A(Y0 + 44 * K, 1024, BF16)
        assert Y0 + 46 * K <= TOP
        mset("pool", hT2[:, :, 0:1], 0.0, ["h2pad"])
        mset("pool", hT2[:, :, SP2 - 1:SP2], 0.0, ["h2pad"])
        WR = 3
        wup = [A(N0 + i * 2 * K, 8 * 128, BF16).rearrange("p (c n) -> p c n", c=8) for i in range(WR)]
        for n_, c0_ in ((0, 0), (1, DFF)):
            dma_pool(wup[n_], w_up[:, c0_:c0_ + 128].rearrange("(c p) n -> p c n", p=128), writes=[("wup", n_)], grp=f"wup{n_}")

        def p5_s1(i):
            norm_s1(i, x2[:, i, :], [("x2", i, 0), ("x2", i, 1)], 2, G2, MOD["G2"], sh2, MOD["sh2"], t1_c, hb_c, junk_c)

        dma_sp(xt4[0], x[0:128, :], writes=[("xt4", 0)], grp="xt0")
        for i in range(NT):
            b3 = i % 2
            if i + 1 < NT:
                dma_sp(xt4[(i + 1) % 2], x[(i + 1) * 128:(i + 2) * 128, :], writes=[("xt4", (i + 1) % 2)], grp=f"xt{(i + 1) % 2}")
            for dh in range(2):
                bank = (2 * i + dh) % 4
                ds = slice(dh * 512, (dh + 1) * 512)
                for ec in range(8):
                    src = yfT if ec < 4 else ynaT
                    mm(ps[:, bank, :], src[:, ec % 4, i * 128:(i + 1) * 128], wo[:, ec, ds], ec == 0, ec == 7,
                       reads=[("wo", ec)], writes=[("ps", bank)])
                tt("dve", x2[:, i, ds], ps[:, bank, :], xt4[b3][:, ds], ALU.add,
                   reads=[("xt4", b3)], xreads=[("ps", bank)], writes=[("x2", i, dh)])
            if i >= 2:
                norm_s2(i - 2, hT2, "hT2", 1, hb_c, (6, 7))
            p5_s1(i)
        norm_s2(NT - 2, hT2, "hT2", 1, hb_c, (6, 7))
        norm_s2(NT - 1, hT2, "hT2", 1, hb_c, (6, 7))
        dump("x2", x2, [128, NT, 1024], F32, [("x2", i, dh) for i in range(NT) for dh in range(2)])
        dump("hT2", hT2, [128, 8, SP2], BF16, [("hT2", i) for i in range(NT)] + ["h2pad"])
        p5_last = [P.ops[e][-1] for e in ("pe", "act", "dve", "pool")]
        wdst = [A(N0 + 6 * K + i * 4 * K, 1024, F32) for i in range(2)]
        AC0 = N0 + 14 * K
        accg = [A(AC0 + i * 4352, 1026, F32).rearrange("p (b n) -> p b n", b=3) for i in range(2)]
        accv = [A(AC0 + 8704 + i * 4352, 1026, F32).rearrange("p (b n) -> p b n", b=3) for i in range(2)]
        assert AC0 + 4 * 4352 <= 64 * K, AC0
        junk_d = A(AC0, 1024, BF16)
        aT = [A(128 * K + i * 16 * K, 4 * S, BF16).rearrange("p (c t) -> p c t", c=4) for i in range(2)]
        wd = [A(160 * K + i * 8 * K, 4 * 1024, BF16).rearrange("p (c n) -> p c n", c=4) for i in range(2)]
        stage = [A(TOP + i * 2 * K, 512, F32) for i in range(2)]
        sg = [A(TOP + 4 * K + i * 2176, 1026, BF16).rearrange("p (b n) -> p b n", b=3) for i in range(2)]
        gfin = A(TOP + 12 * K, 1024, F32)

        fpieces = [list(range(s, min(s + 4, NFC))) for s in range(0, NFC, 4)]
        NP = len(fpieces)
        OFFS = [0, 341, 682]
        NCH = 2 * NFC
        ucnt = [0]
        ecnt = [0]
        ccnt = [0]
        scnt = [0]

        def chunk_col(n):
            return (n // 2) * 128 if n % 2 == 0 else DFF + (n // 2) * 128

        def wup_load(n):
            return

        def wup_cast(n):
            if n >= NCH:
                return
            c0 = chunk_col(n)
            dma_pool(wup[n % WR], w_up[:, c0:c0 + 128].rearrange("(c p) n -> p c n", p=128), writes=[("wup", n % WR)], grp=f"wup{n % WR}")


        def up_unit(wb, hf):
            u = ucnt[0] % 2
            ucnt[0] += 1
            rds = [("wup", wb), "h2pad"] + [("hT2", t) for t in range(8 * hf, 8 * hf + 8)] + [("hT2", 8 if hf == 0 else 7)]
            for b in range(3):
                bank = 3 * u + b
                c0 = hf * 1024 + OFFS[b]
                for kc in range(8):
                    mm(ps[:, bank, 0:344], wup[wb][:, kc, :], hT2[:, kc, c0:c0 + 344], kc == 0, kc == 7, reads=rds,
                       writes=[("psu", u), ("ps", bank)])
            return u

        def conv_unit(u, ch, acc, accname):
            pu = ps[:, 3 * u:3 * u + 3, :]
            ccnt[0] += 1
            P.add("act", (lambda o, i_, sc, bi: (lambda e: e.activation(out=o, in_=i_, func=AF.Identity, scale=sc, bias=bi)))(
                acc, pu[:, :, 1:343], convw[:, ch, 1:2], convb[:, ch:ch + 1]),
                reads=["convw", "convb"], xreads=[("psu", u)], writes=[accname], extra=(p5_last if ccnt[0] <= 4 else ()))
            stt("dve", acc, pu[:, :, 0:342], convw[:, ch, 0:1], acc, ALU.mult, ALU.add,
                reads=["convw", accname], xreads=[("psu", u)], writes=[accname])
            stt("dve", acc, pu[:, :, 2:344], convw[:, ch, 2:3], acc, ALU.mult, ALU.add,
                reads=["convw", accname], xreads=[("psu", u)], writes=[accname])

        pending = []

        def drain(k):
            for _ in range(k):
                if pending:
                    pending.pop(0)()

        def up_piece(pi, ndrain):
            ab = pi % 2
            for ci, j in enumerate(fpieces[pi]):
                wg = (2 * j) % WR
                wv = (2 * j + 1) % WR
                for hf in range(2):
                    k2 = hf
                    ug = up_unit(wg, hf)
                    conv_unit(ug, j, accg[k2], ("accg", k2))
                    wup_cast(2 * j + 2 + hf)
                    wup_load(2 * j + 4 + hf)
                    if ci == len(fpieces[pi]) - 1:
                        wd_dma(pi + 1, hf)
                    drain(ndrain)
                    uv = up_unit(wv, hf)
                    conv_unit(uv, NFC + j, accv[k2], ("accv", k2))
                    drain(ndrain)
                    act(sg[k2], accg[k2], AF.Silu, reads=[("accg", k2)], writes=[("sg", k2)])
                    base = hf * 1024
                    for b in range(3):
                        P.add("dve", (lambda o, i0, i1: (lambda e: e.tensor_tensor(out=o, in0=i0, in1=i1, op=ALU.mult)))(
                            aT[ab][:, ci, base + OFFS[b]:base + OFFS[b] + 342], sg[k2][:, b, :], accv[k2][:, b, :]),
                            reads=[("sg", k2), ("accv", k2)], writes=[("aT", ab, ci, hf)], extra=(p5_last if pi == 1 else ()))

        wd_pref = set()

        def wd_dma(pi, ci):
            if pi >= NP or ci >= len(fpieces[pi]) or (pi, ci) in wd_pref:
                return
            wd_pref.add((pi, ci))
            ex = p5_last if pi < 2 else ()
            j = fpieces[pi][ci]
            b2 = ci % 2
            P.add("sp", (lambda o, i_: (lambda e: e.dma_start(out=o, in_=i_)))(wdst[b2], w_down[j * 128:(j + 1) * 128, :]),
                  writes=[("wdst", b2)], dma=f"wdst{b2}", extra=ex)

        def load_wd(pi):
            ab = pi % 2
            ex = p5_last if pi < 2 else ()
            wd_dma(pi, 0)
            wd_dma(pi, 1)
            for ci, j in enumerate(fpieces[pi]):
                b2 = ci % 2
                P.add("pool", (lambda o, i0: (lambda e: e.tensor_tensor(out=o, in0=i0, in1=gt2, op=ALU.mult)))(wd[ab][:, ci, :], wdst[b2]),
                      reads=[("wdst", b2)] + MOD["gt2"], writes=[("wd", ab, ci)], extra=ex)
                wd_dma(pi, ci + 2)

        def down_group(pi, t, dh):
            ab = pi % 2
            n = len(fpieces[pi])
            bank = 6 + dh
            ds = slice(dh * 512, (dh + 1) * 512)
            for ci in range(n):
                mm(ps[:, bank, :], aT[ab][:, ci, t * 128:(t + 1) * 128], wd[ab][:, ci, ds], ci == 0, ci == n - 1,
                   reads=[("aT", ab, ci, t // 8), ("wd", ab, ci)], writes=[("ps", bank)])
            if dh == 1 and (pi != NP - 2 or t % 2 == 1):
                tt("dve", x2[:, t, ds], ps[:, bank, :], x2[:, t, ds], ALU.add,
                   reads=[("x2", t, dh)], xreads=[("ps", bank)], writes=[("x2", t, dh)])
            else:
                sb2 = scnt[0] % 2
                scnt[0] += 1
                act(stage[sb2], ps[:, bank, :], AF.Copy, xreads=[("ps", bank)], writes=[("stage", sb2)])
                tt("pool", x2[:, t, ds], x2[:, t, ds], stage[sb2], ALU.add,
                   reads=[("stage", sb2), ("x2", t, dh)], writes=[("x2", t, dh)])
            if pi == NP - 1 and dh == 1:
                if t > 0:
                    final_norm(t - 1)
                if t == NT - 1:
                    final_norm(t)

        def final_norm(t):
            xr = [("x2", t, 0), ("x2", t, 1)]
            act(junk_d, x2[:, t, :], AF.Square, reads=xr, writes=["junk", ("ss", 3, t)], accum=ss[:, 3, t:t + 1])
            rstd_of(3, t, D)
            stt("dve", x2[:, t, :], x2[:, t, :], rstd[:, 3, t:t + 1], gfin, ALU.mult, ALU.mult,
                reads=xr + [("rstd", 3, t), "gfin"], writes=xr)
            P.add("sp", lambda e: e.dma_start(out=y_out[t * 128:(t + 1) * 128, :], in_=x2[:, t, :]), reads=xr, dma="out")

        def queue_down(pi):
            for t in range(NT):
                for dh in range(2):
                    pending.append(lambda pi=pi, t=t, dh=dh: down_group(pi, t, dh))

        load_wd(0)
        P.add("sp", lambda e: e.dma_start(out=gfin, in_=g_fin_bc), writes=["gfin"], dma="gfin", extra=p5_last)
        up_piece(0, 0)
        for pi in range(1, NP):
            load_wd(pi)
            queue_down(pi - 1)
            nunits = 4 * len(fpieces[pi])
            up_piece(pi, -(-32 // nunits))
            drain(len(pending))
        queue_down(NP - 1)
        drain(len(pending))

        P.emit(nc, final_dma_groups=["out"])
    return nc, dbg


_NC_CACHE = None


def kernel(**inputs):
    global _NC_CACHE
    inp = {k: np.asarray(v) for k, v in inputs.items()}
    maps = _layout_inputs(inp)
    if _NC_CACHE is None:
        _NC_CACHE = build_nc()
    nc, dbg = _NC_CACHE
    res = run_bass_kernel_spmd(nc, maps, core_ids=list(range(8)))
    out = np.stack([np.asarray(res.results[b]["y"], dtype=np.float32) for b in range(8)], axis=0)
    if DEBUG:
        kernel.debug = [{k: np.asarray(res.results[b]["dbg_" + k]) for k in dbg} for b in range(8)]
    return out
```

```python
import math
from contextlib import ExitStack

import numpy as np
import ml_dtypes
import concourse.bass as bass
import concourse.mybir as mybir
from concourse.bass_utils import run_bass_kernel_spmd

F32 = mybir.dt.float32
BF16 = mybir.dt.bfloat16
AF = mybir.ActivationFunctionType
ALU = mybir.AluOpType

D = 1024
S = 2048
NT = 16
DFF = 2816
NFC = 22
EPS = 1e-6
ENGS = ["pe", "act", "dve", "pool", "sp"]
DEBUG = []


class Op:
    __slots__ = ("eng", "fn", "deps", "signal", "seq", "dma", "twrites")

    def __init__(self, eng, fn, dma):
        self.eng = eng
        self.fn = fn
        self.deps = []
        self.signal = False
        self.seq = 0
        self.dma = dma
        self.twrites = ()


class Prog:
    def __init__(self):
        self.ops = {e: [] for e in ENGS}
        self.last_w = {}
        self.readers = {}
        self.dma_groups = {}
        self.bar = None
        self.bar_done = set()

    def barrier(self):
        lasts = []
        for e in ENGS:
            for op in reversed(self.ops[e]):
                if op.dma is None:
                    lasts.append(op)
                    break
        seen = set()
        for e in ENGS:
            for op in reversed(self.ops[e]):
                if op.dma is not None and op.dma not in seen:
                    seen.add(op.dma)
                    lasts.append(op)
        self.bar = lasts
        self.bar_done = set()

    def add(self, eng, fn, reads=(), writes=(), xreads=(), dma=None, extra=()):
        op = Op(eng, fn, dma)
        op.twrites = tuple(writes)
        deps = {}

        def consider(d, raw):
            if d is None or d is op:
                return
            if d.eng == eng and d.dma is None and not raw:
                return
            deps[id(d)] = d

        for r in reads:
            consider(self.last_w.get(r), True)
        for r in xreads:
            w = self.last_w.get(r)
            consider(w, w is not None and r in w.twrites)
            for rd in self.readers.get(r, ()):
                consider(rd, False)
        for r in writes:
            consider(self.last_w.get(r), False)
            for rd in self.readers.get(r, ()):
                consider(rd, False)
        for d in extra:
            consider(d, True)
        if self.bar is not None and eng not in self.bar_done:
            self.bar_done.add(eng)
            for d in self.bar:
                consider(d, True)
        op.deps = [(d, self.dma_groups[d.dma]["count"] if d.dma is not None else 0) for d in deps.values()]
        for d, _ in op.deps:
            d.signal = True
        for r in reads:
            self.readers.setdefault(r, []).append(op)
        for r in list(writes) + list(xreads):
            self.last_w[r] = op
            self.readers[r] = []
        if dma is not None:
            g = self.dma_groups.setdefault(dma, {"eng": eng, "count": 0})
            assert g["eng"] == eng, (dma, g["eng"], eng)
            g["count"] += 1
        self.ops[eng].append(op)
        return op

    def emit(self, nc, final_dma_groups):
        with ExitStack() as st:
            esem = {e: st.enter_context(nc.semaphore("s_" + e)) for e in ENGS}
            dsem = {g: st.enter_context(nc.semaphore("d_" + g)) for g in self.dma_groups}
            for e in ENGS:
                c = 0
                for op in self.ops[e]:
                    if op.dma is None and op.signal:
                        c += 1
                        op.seq = c
            block = st.enter_context(nc.Block())

            def run(e, eng):
                known = {}
                for op in self.ops[e]:
                    need = {}
                    for d, cnt in op.deps:
                        if d.dma is not None:
                            k = ("d", d.dma)
                            v = 16 * cnt
                        else:
                            k = ("e", d.eng)
                            v = d.seq
                        if v > need.get(k, 0):
                            need[k] = v
                    for k, v in need.items():
                        if known.get(k, 0) >= v:
                            continue
                        known[k] = v
                        eng.wait_ge(dsem[k[1]] if k[0] == "d" else esem[k[1]], v)
                    ins = op.fn(eng)
                    if op.dma is not None:
                        ins.then_inc(dsem[op.dma], 16)
                    elif op.signal:
                        ins.then_inc(esem[e], 1)
                if e == "sp":
                    for g in final_dma_groups:
                        eng.wait_ge(dsem[g], 16 * self.dma_groups[g]["count"])

            @block.tensor
            def _(eng):
                run("pe", eng)

            @block.scalar
            def _(eng):
                run("act", eng)

            @block.vector
            def _(eng):
                run("dve", eng)

            @block.gpsimd
            def _(eng):
                run("pool", eng)

            @block.sync
            def _(eng):
                run("sp", eng)


def _attn_variant(i):
    def rs(r):
        return min(max(r - 4, 0), 24)
    r0 = 2 * i
    jb = rs(r0) // 2
    p = np.arange(128)
    half = p // 64
    kc = p % 64
    qc = np.arange(64)
    cstart = np.clip(qc - 8, 0, 48)
    vcol = (kc[:, None] >= cstart[None, :]) & (kc[:, None] < cstart[None, :] + 16)
    dc = np.clip(kc[:, None] - qc[None, :], -15, 15) + 15
    drA = np.zeros((128, 4, 2, 64), np.int64)
    mA = np.zeros((128, 4, 2, 64), bool)
    for c in range(4):
        kr = 2 * (jb + c) + half
        for qh in range(2):
            r = r0 + qh
            vrow = (kr >= rs(r)) & (kr <= rs(r) + 7)
            drA[:, c, qh, :] = np.clip(kr - r, -7, 7)[:, None] + 7
            mA[:, c, qh, :] = vrow[:, None] & vcol
    dcA = np.broadcast_to(dc[:, None, None, :], (128, 4, 2, 64))
    extra = rs(r0 + 1) % 2 == 1
    drB = mB = None
    if extra:
        kr = 2 * (jb + 4) + half
        r = r0 + 1
        vrow = (kr >= rs(r)) & (kr <= rs(r) + 7)
        drB = np.broadcast_to(np.clip(kr - r, -7, 7)[:, None] + 7, (128, 64))
        mB = vrow[:, None] & vcol
    return jb, extra, drA, dcA, mA, drB, dc, mB


VARIANT_OF_TILE = {0: 1, 1: 2, 14: 3, 15: 4}


def _host_constants():
    c = {}
    c["ident"] = np.eye(128, dtype=ml_dtypes.bfloat16)
    n = np.arange(128)
    ang = 2.0 * np.pi * ((n[:, None] * n[None, :]) % 128) / 128.0
    c["cc"] = (np.cos(ang) / 512.0).astype(ml_dtypes.bfloat16)
    c["sc"] = (np.sin(ang) / 512.0).astype(ml_dtypes.bfloat16)
    n1 = np.arange(128)
    n2 = np.arange(16)
    k1 = np.arange(128)
    tt = 16 * n1[:, None] + n2[None, :]
    a1 = 2.0 * np.pi * ((tt[:, :, None] * k1[None, None, :]) % S) / S
    c["mc"] = np.cos(a1).astype(ml_dtypes.bfloat16)
    c["ms"] = np.sin(a1).astype(ml_dtypes.bfloat16)
    c["nms"] = (-np.sin(a1)).astype(ml_dtypes.bfloat16)
    a2 = 2.0 * np.pi * np.outer(np.arange(16), np.arange(16)) / 16.0
    eye8 = np.eye(8)
    c["bc"] = np.kron(np.cos(a2), eye8).astype(ml_dtypes.bfloat16)
    c["bsn"] = np.kron(-np.sin(a2), eye8).astype(ml_dtypes.bfloat16)
    c["ones"] = np.ones((128, 128), dtype=ml_dtypes.bfloat16)
    return c


_CONST = None


def _layout_inputs(inp):
    global _CONST
    if _CONST is None:
        _CONST = _host_constants()
    f32 = np.float32

    def bc(v):
        return np.ascontiguousarray(np.broadcast_to(np.asarray(v, f32)[None, :], (128, v.shape[0])))

    def col(v, nch):
        return np.ascontiguousarray(np.asarray(v, f32).reshape(nch, 128).T)

    rpb = np.asarray(inp["rpb"][0], f32)
    biasA = np.zeros((5, 128, 8, 512), f32)
    maskA = np.zeros((5, 128, 512), f32)
    biasB = np.zeros((128, 8, 64), f32)
    maskB = np.zeros((128, 64), f32)
    for i, v in [(5, 0), (0, 1), (1, 2), (14, 3), (15, 4)]:
        jb, extra, drA, dcA, mA, drB, dcB, mB = _attn_variant(i)
        g = rpb[:, drA, dcA]
        biasA[v] = g.reshape(8, 128, 512).transpose(1, 0, 2)
        maskA[v] = mA.reshape(128, 512).astype(f32)
        if v == 0:
            gb = rpb[:, drB, dcB]
            biasB[:] = gb.transpose(1, 0, 2)
            maskB[:] = mB.astype(f32)
    shared = dict(_CONST)
    shared.update({
        "w_ada": np.ascontiguousarray(inp["w_ada"][0], f32),
        "b_ada_bc": bc(inp["b_ada"][0]),
        "g_mix_bc": bc(inp["g_mix"][0]),
        "g_ffn_bc": bc(inp["g_ffn"][0]),
        "g_fin_bc": bc(inp["g_final"]),
        "w_in": np.ascontiguousarray(inp["w_in"][0], f32),
        "w_four": np.ascontiguousarray(inp["w_four"][0], f32),
        "w_out": np.ascontiguousarray(inp["w_out"][0], f32),
        "w_up": np.ascontiguousarray(inp["w_up"][0], f32),
        "w_down": np.ascontiguousarray(inp["w_down"][0], f32),
        "gout_col": np.ascontiguousarray(np.concatenate([col(inp["g_four_out"][0], 4), col(inp["g_na_out"][0], 4)], axis=1)),
        "convw_col": np.ascontiguousarray(np.asarray(inp["conv_w"][0], f32).reshape(3, 44, 128).transpose(2, 1, 0)),
        "convb_col": col(inp["conv_b"][0], 44),
        "biasA": biasA, "maskA": maskA, "biasB": biasB, "maskB": maskB,
    })
    maps = []
    for b in range(8):
        m = dict(shared)
        m["x"] = np.ascontiguousarray(inp["x"][b], f32)
        m["ccol"] = col(inp["c"][b], 8)
        maps.append(m)
    return maps


def build_nc():
    nc = bass.Bass("TRN2", target_bir_lowering=False)

    def din(name, shape, dt=F32):
        return nc.dram_tensor(name, shape, dt, kind="ExternalInput").ap()

    x = din("x", [S, D])
    ccol = din("ccol", [128, 8])
    w_ada = din("w_ada", [D, 6 * D])
    b_ada_bc = din("b_ada_bc", [128, 6 * D])
    g_mix_bc = din("g_mix_bc", [128, D])
    g_ffn_bc = din("g_ffn_bc", [128, D])
    g_fin_bc = din("g_fin_bc", [128, D])
    w_in = din("w_in", [D, 2048])
    w_four = din("w_four", [4, 128, 128])
    w_out = din("w_out", [D, D])
    w_up = din("w_up", [D, 2 * DFF])
    w_down = din("w_down", [DFF, D])
    gout_col_d = din("gout_col", [128, 8])
    convw_d = din("convw_col", [128, 44, 3])
    convb_d = din("convb_col", [128, 44])
    biasA_d = din("biasA", [5, 128, 8, 512])
    maskA_d = din("maskA", [5, 128, 512])
    biasB_d = din("biasB", [128, 8, 64])
    maskB_d = din("maskB", [128, 64])
    ident_d = din("ident", [128, 128], BF16)
    cc_d = din("cc", [128, 128], BF16)
    sc_d = din("sc", [128, 128], BF16)
    mc_d = din("mc", [128, 16, 128], BF16)
    ms_d = din("ms", [128, 16, 128], BF16)
    nms_d = din("nms", [128, 16, 128], BF16)
    bc_d = din("bc", [128, 128], BF16)
    bsn_d = din("bsn", [128, 128], BF16)
    ones_d = din("ones", [128, 128], BF16)
    y_out = nc.dram_tensor("y", [S, D], F32, kind="ExternalOutput").ap()
    dbg = {}

    P = Prog()
    with ExitStack() as st:
        ARENA_F32 = 51712
        arena = st.enter_context(nc.sbuf_tensor("arena", [128, ARENA_F32], F32))
        ps = st.enter_context(nc.psum_tensor("ps", [128, 8, 512], F32))

        def A(off, n_elem, dt):
            assert off % 4 == 0
            nb = n_elem * (2 if dt == BF16 else 4)
            assert nb % 4 == 0 and off + nb <= ARENA_F32 * 4, (off, nb)
            v = arena[:, off // 4:(off + nb) // 4]
            return v.bitcast(BF16) if dt == BF16 else v

        def sb(name, shape, dt):
            return st.enter_context(nc.sbuf_tensor(name, shape, dt))

        K = 1024
        ident = sb("ident_sb", [128, 128], BF16)
        ones = sb("ones_sb", [128, 128], BF16)
        c_sb = sb("c_sb", [128, 8], F32)
        epsb = sb("epsb", [128, 1], F32)
        ss = sb("ss", [128, 5, NT], F32)
        rstd = sb("rstd", [128, 5, NT], F32)
        gout_col = sb("gout_col_sb", [128, 8], F32)
        convw = sb("convw_sb", [128, 44, 3], F32)
        convb = sb("convb_sb", [128, 44], F32)
        rden = sb("rden", [128, 2, 8], F32)

        TOP = 176 * K
        G1 = A(TOP + 0 * K, 1024, F32)
        sh1 = A(TOP + 4 * K, 1024, F32)
        gt1 = A(TOP + 8 * K, 1024, F32)
        G2 = A(TOP + 12 * K, 1024, F32)
        sh2 = A(TOP + 16 * K, 1024, F32)
        gt2 = A(TOP + 22 * K, 1024, F32)

        def dma_sp(out, in_, writes=(), reads=(), grp="c0"):
            return P.add("sp", lambda e: e.dma_start(out=out, in_=in_), reads=reads, writes=writes, dma=grp)

        def dma_pool(out, in_, writes=(), reads=(), grp="p0"):
            return P.add("pool", lambda e: e.dma_start(out=out, in_=in_), reads=reads, writes=writes, dma=grp)

        def mm(out, lhsT, rhs, start, stop, reads, writes):
            return P.add("pe", lambda e: e.matmul(out, lhsT=lhsT, rhs=rhs, start=start, stop=stop), reads=reads, writes=writes)

        def tr(out, in_, reads, writes):
            return P.add("pe", lambda e: e.transpose(out=out, in_=in_, identity=ident[:]), reads=list(reads) + ["ident"], writes=writes)

        def act(out, in_, func, reads=(), writes=(), xreads=(), scale=None, bias=None, accum=None):
            kw = {}
            if scale is not None:
                kw["scale"] = scale
            if bias is not None:
                kw["bias"] = bias
            if accum is not None:
                kw["accum_out"] = accum
            return P.add("act", lambda e: e.activation(out=out, in_=in_, func=func, **kw), reads=reads, writes=writes, xreads=xreads)

        def tt(eng, out, in0, in1, op, reads=(), writes=(), xreads=()):
            return P.add(eng, lambda e: e.tensor_tensor(out=out, in0=in0, in1=in1, op=op), reads=reads, writes=writes, xreads=xreads)

        def stt(eng, out, in0, scalar, in1, op0, op1, reads=(), writes=(), xreads=()):
            return P.add(eng, lambda e: e.scalar_tensor_tensor(out=out, in0=in0, scalar=scalar, in1=in1, op0=op0, op1=op1),
                         reads=reads, writes=writes, xreads=xreads)

        def tsc(eng, out, in0, s1, op0, reads=(), writes=(), xreads=()):
            return P.add(eng, lambda e: e.tensor_scalar(out=out, in0=in0, scalar1=s1, scalar2=None, op0=op0),
                         reads=reads, writes=writes, xreads=xreads)

        def recip(out, in_, reads=(), writes=(), xreads=()):
            return P.add("dve", lambda e: e.reciprocal(out=out, in_=in_), reads=reads, writes=writes, xreads=xreads)

        def cpy(eng, out, in_, reads=(), writes=(), xreads=()):
            return P.add(eng, lambda e: e.tensor_copy(out=out, in_=in_), reads=reads, writes=writes, xreads=xreads)

        def mset(eng, ap, val, writes):
            return P.add(eng, lambda e: e.memset(ap, val), writes=writes)

        def dump(name, ap_sb, shape, dt, reads):
            if name not in DEBUG:
                return
            t = nc.dram_tensor("dbg_" + name, shape, dt, kind="ExternalOutput").ap()
            dbg[name] = t
            P.add("sp", lambda e: e.dma_start(out=t, in_=ap_sb), reads=reads, dma="out")

        def rstd_of(sidx, i, n):
            act(rstd[:, sidx, i:i + 1], ss[:, sidx, i:i + 1], AF.Sqrt, reads=[("ss", sidx, i), "epsb"], writes=[("rstd", sidx, i)],
                scale=1.0 / n, bias=epsb[:, 0:1])
            recip(rstd[:, sidx, i:i + 1], rstd[:, sidx, i:i + 1], reads=[("rstd", sidx, i)], writes=[("rstd", sidx, i)])

        cs_rep = sb("cs_rep", [128, 8, 128], BF16)

        dma_sp(ident[:], ident_d, writes=["ident"])
        dma_sp(c_sb[:], ccol, writes=["c_sb"])
        dma_sp(gout_col[:], gout_col_d, writes=["gout_col"])
        dma_sp(convw[:], convw_d, writes=["convw"])
        dma_sp(convb[:], convb_d, writes=["convb"])
        mset("dve", epsb[:], EPS, ["epsb"])
        act(cs_rep[:], c_sb[:].unsqueeze(2).broadcast_to([128, 8, 128]), AF.Silu, reads=["c_sb"], writes=["cs_rep"])
        mpieces = [(1, G1, g_mix_bc), (0, sh1, None), (2, gt1, None), (4, G2, g_ffn_bc), (3, sh2, None), (5, gt2, None)]
        mcast = [0]

        def mod_piece(pi, w32, wbf, bada_b, gtmp_b, tag, banks, ncols=1024, dq=None):
            blk, dst, gsrc = mpieces[pi]
            dq = dq or dma_sp
            dq(bada_b, b_ada_bc[:, blk * 1024:(blk + 1) * 1024], writes=[("bada", tag)], grp=f"bada{tag}")
            if gsrc is not None:
                dq(gtmp_b, gsrc, writes=[("gtmp", tag)], grp=f"bada{tag}")
            for rnd in range(1024 // ncols):
                c0 = blk * 1024 + rnd * ncols
                for kc in range(8):
                    dq(w32[:, kc, :], w_ada[kc * 128:(kc + 1) * 128, c0:c0 + ncols],
                       writes=[("w32", tag, kc)], grp=f"w32{tag}_{kc}")
                for kc in range(8):
                    eng = "dve" if mcast[0] % 3 != 2 else "act"
                    mcast[0] += 1
                    if eng == "dve":
                        cpy("dve", wbf[:, kc, :], w32[:, kc, :], reads=[("w32", tag, kc)], writes=[("wada", tag, kc)])
                    else:
                        act(wbf[:, kc, :], w32[:, kc, :], AF.Copy, reads=[("w32", tag, kc)], writes=[("wada", tag, kc)])
                for h2 in range(ncols // 512):
                    hh = rnd * (ncols // 512) + h2
                    bank = banks[hh]
                    hs = slice(hh * 512, (hh + 1) * 512)
                    for kc in range(8):
                        mm(ps[:, bank, :], cs_rep[:, kc, :], wbf[:, kc, h2 * 512:(h2 + 1) * 512], kc == 0, kc == 7,
                           reads=["cs_rep", ("wada", tag, kc)], writes=[("ps", bank)])
                    dname = ("mod", pi, hh)
                    tt("dve", dst[:, hs], ps[:, bank, :], bada_b[:, hs], ALU.add,
                       reads=[("bada", tag)], xreads=[("ps", bank)], writes=[dname])
                    if gsrc is not None:
                        stt("dve", dst[:, hs], dst[:, hs], 1.0, gtmp_b[:, hs], ALU.add, ALU.mult,
                            reads=[dname, ("gtmp", tag)], writes=[dname])

        w32a = [A(64 * K + i * 32 * K, 8 * 1024, F32).rearrange("p (c n) -> p c n", c=8) for i in range(2)]
        wbfa = [A(128 * K + i * 16 * K, 8 * 1024, BF16).rearrange("p (c n) -> p c n", c=8) for i in range(2)]
        badaa = [A(160 * K + i * 4 * K, 1024, F32) for i in range(2)]
        gtmpa = [A(168 * K + i * 4 * K, 1024, F32) for i in range(2)]
        for pi in range(2):
            b = pi % 2
            mod_piece(pi, w32a[b], wbfa[b], badaa[b], gtmpa[b], b, (2 * b, 2 * b + 1))
        win = A(32 * K, 8 * 2048, BF16).rearrange("p (c n) -> p c n", c=8)
        wi32 = [A(i * 8 * K, 2048, F32) for i in range(4)]
        for kc in range(8):
            b4 = kc % 4
            dma_sp(wi32[b4], w_in[kc * 128:(kc + 1) * 128, :], writes=[("wi32", b4)], grp=f"wi32{b4}")
            cpy("dve", win[:, kc, :], wi32[b4], reads=[("wi32", b4)], writes=[("win", kc)])
        MOD = {name: [("mod", pi, 0), ("mod", pi, 1)] for pi, name in enumerate(["G1", "sh1", "gt1", "G2", "sh2", "gt2"])}
        P.barrier()

        hT = A(0, 8 * S, BF16).rearrange("p (c t) -> p c t", c=8)
        qT = A(64 * K, 4 * S, BF16).rearrange("p (c t) -> p c t", c=4)
        kT = A(80 * K, 4 * S, BF16).rearrange("p (c t) -> p c t", c=4)
        VO = 96 * K
        Vext = A(VO, NT * 8 * 65, BF16).rearrange("p (t h d) -> p t h d", t=NT, h=8)
        UO = VO + 16640 + 256
        uT = A(UO, 4 * S, BF16).rearrange("p (c t) -> p c t", c=4)
        XO = UO + 16 * K
        xt = [A(XO + i * 4 * K, 1024, F32) for i in range(3)]
        t1_a = [A(XO + 12 * K + i * 4 * K, 1024, F32) for i in range(2)]
        hb_a = [A(XO + 20 * K + i * 2 * K, 1024, BF16) for i in range(2)]
        junk_a = A(XO + 24 * K, 1024, BF16)
        assert XO + 26 * K <= TOP

        mset("dve", Vext[:, :, :, 64:65], 1.0, ["vones"])

        def norm_s1(i, src_ap, src_res, sidx, Gt, Gres, sht, shres, t1, hb, junk, pool_ok=True):
            b2 = i % 2
            act(junk, src_ap, AF.Square, reads=src_res, writes=["junk", ("ss", sidx, i)], accum=ss[:, sidx, i:i + 1])
            rstd_of(sidx, i, D)
            stt("dve", t1[b2], src_ap, rstd[:, sidx, i:i + 1], Gt, ALU.mult, ALU.mult,
                reads=list(src_res) + [("rstd", sidx, i)] + Gres, writes=[("t1", b2)])
            tt("pool" if (pool_ok and i % 2 == 0) else "dve", hb[b2], t1[b2], sht, ALU.add,
               reads=[("t1", b2)] + shres, writes=[("hb", b2)])

        def norm_s2(i, dstT, dst_res, col0, hb, tbanks):
            b2 = i % 2
            bank = tbanks[b2]
            pst = ps[:, bank, :].bitcast(BF16)
            for c in range(8):
                tr(pst[:, c * 128:(c + 1) * 128], hb[b2][:, c * 128:(c + 1) * 128], reads=[("hb", b2)], writes=[("ps", bank)])
            act(dstT[:, :, col0 + i * 128:col0 + (i + 1) * 128], pst.rearrange("p (c t) -> p c t", c=8), AF.Copy,
                xreads=[("ps", bank)], writes=[(dst_res, i)])

        pbank = [0]

        def next_bank():
            b = pbank[0]
            pbank[0] = b + 1 if b < 4 else 0
            return b

        def proj_groups(tb):
            ts_ = slice(tb * 512, (tb + 1) * 512)
            hres = [("hT", 4 * tb + j) for j in range(4)]
            out = []

            def fm(col_base, dst, dst_name, scale, fc):
                bank = next_bank()
                for kc in range(8):
                    mm(ps[:, bank, :], win[:, kc, col_base + fc * 128:col_base + (fc + 1) * 128], hT[:, kc, ts_], kc == 0, kc == 7,
                       reads=[("win", kc)] + hres, writes=[("ps", bank)])
                if scale is None:
                    cpy("dve", dst[:, fc, ts_], ps[:, bank, :], xreads=[("ps", bank)], writes=[(dst_name, fc, tb)])
                else:
                    act(dst[:, fc, ts_], ps[:, bank, :], AF.Copy, xreads=[("ps", bank)], writes=[(dst_name, fc, tb)], scale=scale)

            def vm(tt_):
                bank = next_bank()
                for kc in range(8):
                    mm(ps[:, bank, :], hT[:, kc, tt_ * 128:(tt_ + 1) * 128], win[:, kc, 1536:2048], kc == 0, kc == 7,
                       reads=[("win", kc), ("hT", tt_)], writes=[("ps", bank)])
                act(Vext[:, tt_, :, 0:64], ps[:, bank, :].rearrange("p (h d) -> p h d", h=8), AF.Copy,
                    xreads=[("ps", bank)], writes=[("V", tt_)])

            for col_base, dst, dst_name, scale in ((0, uT, "uT", None), (512, qT, "qT", 0.125), (1024, kT, "kT", None)):
                for fc in range(4):
                    out.append(lambda cb=col_base, d=dst, dn=dst_name, sc=scale, fc=fc: fm(cb, d, dn, sc, fc))
            for tt_ in range(4 * tb, 4 * tb + 4):
                out.append(lambda tt_=tt_: vm(tt_))
            return out

        def p1_s1(i):
            if i >= NT:
                return
            b3 = i % 3
            dma_sp(xt[b3], x[i * 128:(i + 1) * 128, :], writes=[("xt", b3)], grp=f"xt{b3}")
            norm_s1(i, xt[b3], [("xt", b3)], 0, G1, MOD["G1"], sh1, MOD["sh1"], t1_a, hb_a, junk_a, pool_ok=False)

        LS0 = XO + 26 * K
        lwb = [A(LS0 + b * 8 * K, 8 * 512, BF16).rearrange("p (c n) -> p c n", c=8) for b in range(2)]
        lgt = A(LS0 + 16 * K, 1024, F32)
        assert LS0 + 20 * K <= TOP
        LGT = ["lgt", "lgt"]
        lrounds = [(pi, r) for pi in range(2, 6) for r in range(2)]

        def late_dma(k):
            pi, r = lrounds[k]
            blk, dst, gsrc = mpieces[pi]
            if r == 0:
                dma_sp(dst, b_ada_bc[:, blk * 1024:(blk + 1) * 1024], writes=[("mod", pi, 0), ("mod", pi, 1)], grp=f"lb{pi}")
                if gsrc is not None:
                    dma_sp(lgt, gsrc, writes=["lgt"], grp=f"lb{pi}")
            c0 = blk * 1024 + r * 512
            for kc in range(8):
                dma_pool(lwb[k % 2][:, kc, :], w_ada[kc * 128:(kc + 1) * 128, c0:c0 + 512],
                         writes=[("lwb", k % 2, kc)], grp=f"lw{k % 2}_{kc}")

        def late_round(k):
            pi, r = lrounds[k]
            blk, dst, gsrc = mpieces[pi]
            hs = slice(r * 512, (r + 1) * 512)
            for kc in range(8):
                mm(ps[:, 5, :], cs_rep[:, kc, :], lwb[k % 2][:, kc, :], kc == 0, kc == 7,
                   reads=["cs_rep", ("lwb", k % 2, kc)], writes=[("ps", 5)])
            tt("dve", dst[:, hs], ps[:, 5, :], dst[:, hs], ALU.add, reads=[("mod", pi, r)], xreads=[("ps", 5)], writes=[("mod", pi, r)])
            if gsrc is not None:
                stt("dve", dst[:, hs], dst[:, hs], 1.0, lgt[:, hs], ALU.add, ALU.mult, reads=[("mod", pi, r), LGT[r]], writes=[("mod", pi, r)])

        def late_hook(j):
            if j % 2 == 1 and j >= 3:
                late_round((j - 3) // 2)
            if j % 2 == 0:
                late_dma(j // 2)

        p1_s1(0)
        p1_s1(1)
        for i in range(4):
            norm_s2(i, hT, "hT", 0, hb_a, (6, 7))
            p1_s1(i + 2)
            late_hook(i)
        for tb in range(4):
            groups = proj_groups(tb)
            for gi, g in enumerate(groups):
                g()
                if gi % 4 == 3 and tb < 3:
                    i = 4 * (tb + 1) + gi // 4
                    norm_s2(i, hT, "hT", 0, hb_a, (6, 7))
                    p1_s1(i + 2)
                    late_hook(i)
        late_round(7)
        dump("hT", hT, [128, 8, S], BF16, [("hT", i) for i in range(NT)])
        dump("qT", qT, [128, 4, S], BF16, [("qT", fc, tb) for fc in range(4) for tb in range(4)])
        dump("kT", kT, [128, 4, S], BF16, [("kT", fc, tb) for fc in range(4) for tb in range(4)])
        dump("uT", uT, [128, 4, S], BF16, [("uT", fc, tb) for fc in range(4) for tb in range(4)])
        dump("V", Vext, [128, NT, 8, 65], BF16, [("V", t) for t in range(NT)] + ["vones"])
        P.barrier()

        EA = A(0, 5 * 8 * 512, BF16).rearrange("p (v h n) -> p v h n", v=5, h=8)
        EB = A(40 * K, 8 * 64, BF16).rearrange("p (h n) -> p h n", h=8)
        bstage = A(41 * K, 8 * 512, F32).rearrange("p (h n) -> p h n", h=8)
        mstage = A(57 * K, 512, F32)
        Y0 = UO + 16 * K
        ynaT = A(Y0, 4 * S, BF16).rearrange("p (c t) -> p c t", c=4)
        yfT = A(Y0 + 16 * K, 4 * S, BF16).rearrange("p (c t) -> p c t", c=4)
        o_n = [A(Y0 + 32 * K + i * 2 * K, 512, F32) for i in range(2)]
        o_bf = [A(Y0 + 36 * K + i * K, 512, BF16) for i in range(2)]
        junk_b = A(Y0 + 38 * K, 512, BF16)
        PB0 = Y0 + 20 * K
        peA = [A(PB0 + i * 2 * K, 1024, BF16).rearrange("p (e n) -> p e n", e=2) for i in range(2)]
        pTA = [A(PB0 + 4 * K + i * 2 * K, 1024, BF16).rearrange("p (e n) -> p e n", e=2) for i in range(2)]
        peB = [A(PB0 + 8 * K + i * 256, 128, BF16).rearrange("p (e n) -> p e n", e=2) for i in range(2)]
        pTB = [A(PB0 + 8 * K + 512 + i * 256, 128, BF16).rearrange("p (e n) -> p e n", e=2) for i in range(2)]
        bstB = A(Y0 + 16 * K, 8 * 64, F32).rearrange("p (h n) -> p h n", h=8)
        mstB = A(Y0 + 18 * K, 64, F32)
        assert Y0 + 39 * K <= TOP

        bsth = [bstage[:, 0:4, :], bstage[:, 4:8, :]]
        ecount = [0]

        def load_E(v):
            dma_sp(mstage, maskA_d[v], writes=["mstage"], grp="mst")
            for hh in range(2):
                dma_sp(bsth[hh], biasA_d[v][:, hh * 4:(hh + 1) * 4, :],
                       writes=[("bstage", hh), ("bsq", 2 * hh), ("bsq", 2 * hh + 1)], grp=f"bst{hh}")

        def build_E0_quarters():
            dma_sp(mstage, maskA_d[0], writes=["mstage"], grp="mst")
            for q in range(4):
                dma_sp(bstage[:, 2 * q:2 * q + 2, :], biasA_d[0][:, 2 * q:2 * q + 2, :], writes=[("bsq", q)], grp=f"bsq{q}")
            dma_sp(bstB, biasB_d, writes=["bstB"], grp="mstB")
            dma_sp(mstB, maskB_d, writes=["mstB"], grp="mstB")
            for q in range(4):
                act(bstage[:, 2 * q:2 * q + 2, :], bstage[:, 2 * q:2 * q + 2, :], AF.Exp, reads=[("bsq", q)], writes=[("bsq", q)])
                tt("dve", EA[:, 0, 2 * q:2 * q + 2, :], bstage[:, 2 * q:2 * q + 2, :], mstage.unsqueeze(1).broadcast_to([128, 2, 512]),
                   ALU.mult, reads=[("bsq", q), "mstage"], writes=[("EA", 0, q)])
                if q == 0:
                    act(bstB, bstB, AF.Exp, reads=["bstB"], writes=["bstB"])
                    tt("dve", EB, bstB, mstB.unsqueeze(1).broadcast_to([128, 8, 64]), ALU.mult,
                       reads=["bstB", "mstB"], writes=["EB"])

        def build_E_quarter(v, q):
            hh = q // 2
            sl = bstage[:, 2 * q:2 * q + 2, :]
            act(sl, sl, AF.Exp, reads=[("bstage", hh)], writes=[("bstage", hh)])
            tt("dve", EA[:, v, 2 * q:2 * q + 2, :], sl, mstage.unsqueeze(1).broadcast_to([128, 2, 512]), ALU.mult,
               reads=[("bstage", hh), "mstage"], writes=[("EA", v, q)])

        def build_E(v, preloaded=False):
            if not preloaded:
                dma_sp(mstage, maskA_d[v], writes=["mstage"], grp="mst")
            for hh in range(2):
                bsl = hh
                if not preloaded:
                    dma_sp(bsth[bsl], biasA_d[v][:, hh * 4:(hh + 1) * 4, :], writes=[("bstage", bsl)], grp=f"bst{bsl}")
                act(bsth[bsl], bsth[bsl], AF.Exp, reads=[("bstage", bsl)], writes=[("bstage", bsl)])
                tt("dve", EA[:, v, hh * 4:(hh + 1) * 4, :], bsth[bsl], mstage.unsqueeze(1).broadcast_to([128, 4, 512]), ALU.mult,
                   reads=[("bstage", bsl), "mstage"], writes=[("EA", v, 2 * hh), ("EA", v, 2 * hh + 1)])
            if v == 0:
                dma_sp(bstB, biasB_d, writes=["bstB"], grp="mstB")
                dma_sp(mstB, maskB_d, writes=["mstB"], grp="mstB")
                act(bstB, bstB, AF.Exp, reads=["bstB"], writes=["bstB"])
                tt("dve", EB, bstB, mstB.unsqueeze(1).broadcast_to([128, 8, 64]), ALU.mult,
                   reads=["bstB", "mstB"], writes=["EB"])

        wf = A(Y0 + 18 * K + 512, 4 * 128, BF16).rearrange("p (g n) -> p g n", g=4)
        ccs = A(Y0 + 19 * K + 512, 256, BF16)
        AB = cs_rep[:].rearrange("p c m -> p (c m)").rearrange("p (g n) -> p g n", g=4)
        def load_AB():
            dma_sp(ccs[:, 0:128], cc_d, writes=["ccs"])
            dma_sp(ccs[:, 128:256], sc_d, writes=["ccs"])
            for g in range(4):
                dma_pool(wf[:, g, :], w_four[g], writes=[("wf", g)], grp=f"wf{g}")

        def build_AB():
            for g in range(4):
                for s2 in range(2):
                    mm(ps[:, 5, s2 * 128:(s2 + 1) * 128], ccs[:, s2 * 128:(s2 + 1) * 128], wf[:, g, :],
                       True, True, reads=["ccs", ("wf", g)], writes=[("ps", 5)])
                act(AB[:, g, :], ps[:, 5, 0:256], AF.Copy, xreads=[("ps", 5)], writes=["AB"])

        build_E0_quarters()

        TORDER = list(range(2, 14)) + [0, 1, 14, 15]
        useq = [(i, hp) for i in TORDER for hp in range(4)]
        tinfo = {i: (_attn_variant(i)[0], _attn_variant(i)[1], VARIANT_OF_TILE.get(i, 0)) for i in range(NT)}
        po = [ps[:, 6 + g, 0:260].rearrange("p (h d) -> p h d", h=4) for g in range(2)]

        def attn_qk(m):
            i, hp = useq[m]
            jb, extra, v = tinfo[i]
            u2 = m % 2
            qs = slice(i * 128, (i + 1) * 128)
            def qk_a(c, e):
                pb = e * 64
                mm(ps[:, u2 + 2 * e, c * 128:(c + 1) * 128], kT[pb:pb + 64, hp, (jb + c) * 128:(jb + c + 1) * 128],
                   qT[pb:pb + 64, hp, qs], True, True,
                   reads=[("kT", hp, (jb + c) // 4), ("qT", hp, i // 4)], writes=[("ps", u2 + 2 * e)])

            def qk_b(e):
                pb = e * 64
                mm(ps[:, 4 + u2, 256 + e * 64:256 + (e + 1) * 64], kT[pb:pb + 64, hp, (jb + 4) * 128:(jb + 5) * 128],
                   qT[pb:pb + 64, hp, i * 128 + 64:(i + 1) * 128], True, True,
                   reads=[("kT", hp, (jb + 4) // 4), ("qT", hp, i // 4)], writes=[("ps", 4 + u2)])

            for e in range(2):
                for c in range(4):
                    qk_a(c, e)
                if extra:
                    qk_b(e)

        def attn_soft(m):
            i, hp = useq[m]
            jb, extra, v = tinfo[i]
            u2 = m % 2
            for e in range(2):
                act(peA[u2][:, e, :], ps[:, u2 + 2 * e, :], AF.Exp, xreads=[("ps", u2 + 2 * e)], writes=[("peA", u2, e)])
                if e == 0 and extra:
                    act(peB[u2], ps[:, 4 + u2, 256:384].rearrange("p (e n) -> p e n", e=2), AF.Exp,
                        xreads=[("ps", 4 + u2)], writes=[("peB", u2)])
            for e in range(2):
                tt("dve", pTA[u2][:, e, :], peA[u2][:, e, :], EA[:, v, 2 * hp + e, :], ALU.mult,
                   reads=[("peA", u2, e), ("EA", v, hp)], writes=[("pTA", u2, e)])
                if e == 0 and extra:
                    tt("dve", pTB[u2], peB[u2], EB[:, 2 * hp:2 * hp + 2, :], ALU.mult, reads=[("peB", u2), "EB"], writes=[("pTB", u2)])

        def attn_pv(m):
            i, hp = useq[m]
            jb, extra, v = tinfo[i]
            u2 = m % 2
            nch = 5 if extra else 4
            for e in range(2):
                h = 2 * hp + e
                pv = po[h // 4]
                hb4 = h % 4
                for c in range(nch):
                    for qh in range(2):
                        if c == 4 and qh == 0:
                            continue
                        last = (c == 3 and not (extra and qh == 1)) or c == 4
                        if c < 4:
                            lhs = pTA[u2][:, e, c * 128 + qh * 64:c * 128 + (qh + 1) * 64]
                            rd = [("pTA", u2, e)]
                        else:
                            lhs = pTB[u2][:, e, :]
                            rd = [("pTB", u2)]
                        mm(pv[qh * 64:(qh + 1) * 64, hb4, :], lhs, Vext[:, jb + c, h, :], c == 0, last,
                           reads=rd + [("V", jb + c), "vones"], writes=[("ps", 6 + h // 4)])

        def attn_out_a(i):
            b2 = i % 2
            for g in range(2):
                recip(rden[:, b2, g * 4:(g + 1) * 4], po[g][:, :, 64], xreads=[("ps", 6 + g)], writes=[("rden", b2, g)])
                tt("dve", o_n[b2][:, g * 256:(g + 1) * 256].rearrange("p (h d) -> p h d", h=4), po[g][:, :, 0:64],
                   rden[:, b2, g * 4:(g + 1) * 4].unsqueeze(2).broadcast_to([128, 4, 64]), ALU.mult,
                   reads=[("rden", b2, g)], xreads=[("ps", 6 + g)], writes=[("o_n", b2, g)])

        def attn_out_a2(i):
            b2 = i % 2
            act(junk_b, o_n[b2], AF.Square, reads=[("o_n", b2, 0), ("o_n", b2, 1)], writes=["junk", ("ss", 1, i)],
                accum=ss[:, 1, i:i + 1])
            act(rstd[:, 1, i:i + 1], ss[:, 1, i:i + 1], AF.Ln, reads=[("ss", 1, i), "epsb"], writes=[("rstd", 1, i)],
                scale=1.0 / 512, bias=epsb[:, 0:1])
            act(rstd[:, 1, i:i + 1], rstd[:, 1, i:i + 1], AF.Exp, reads=[("rstd", 1, i)], writes=[("rstd", 1, i)], scale=-0.5)

        def attn_out_b(i):
            b2 = i % 2
            tsc("dve", o_bf[b2], o_n[b2], rstd[:, 1, i:i + 1], ALU.mult,
                reads=[("o_n", b2, 0), ("o_n", b2, 1), ("rstd", 1, i)], writes=[("o_bf", b2)])
            pst = ps[:, 4, :].bitcast(BF16)
            for c in range(4):
                tr(pst[:, c * 128:(c + 1) * 128], o_bf[b2][:, c * 128:(c + 1) * 128], reads=[("o_bf", b2)], writes=[("ps", 4)])

        def attn_out_c(i):
            pst = ps[:, 4, :].bitcast(BF16)
            cpy("dve", ynaT[:, :, i * 128:(i + 1) * 128], pst[:, 0:512].rearrange("p (c t) -> p c t", c=4),
                xreads=[("ps", 4)], writes=[("ynaT", i)])

        E_SCHED = {(4, 0): 1, (6, 0): 2, (8, 0): 3, (10, 0): 4}
        E_LOAD = {(3, 0): 1, (5, 0): 2, (7, 0): 3, (9, 0): 4}
        attn_qk(0)
        prev = None
        for m in range(len(useq)):
            if m + 1 < len(useq):
                attn_qk(m + 1)
            attn_soft(m)
            attn_pv(m)
            i, hp = useq[m]
            if hp == 3:
                attn_out_a(i)
            if prev is not None and hp == 0:
                attn_out_a2(prev)
            if prev is not None and hp == 1:
                attn_out_b(prev)
            if prev is not None and hp == 2:
                attn_out_c(prev)
            if hp == 3:
                prev = i
            if (i, hp) in E_LOAD:
                load_E(E_LOAD[(i, hp)])
            if (i, 0) in E_SCHED:
                build_E_quarter(E_SCHED[(i, 0)], hp)
            if m == 1:
                load_AB()
            if m == 8:
                build_AB()
        attn_out_a2(prev)

        PQ = A(64 * K, NT * 2 * 4 * 128, BF16).rearrange("p (t s g n) -> p t s g n", t=NT, s=2, g=4)
        Yst = A(0, 2 * 64 * NT * 8, BF16).rearrange("p (s j t e) -> p s j t e", s=2, j=64, t=NT)
        YT = A(32 * K, 64 * 2 * 128, BF16).rearrange("p (j s n) -> p j s n", j=64, s=2)
        yf = A(96 * K, NT * 512, BF16).rearrange("p (t n) -> p t n", t=NT)
        F0 = Y0 + 32 * K
        mct = A(F0 + 2 * K, 16 * 128, BF16).rearrange("p (t n) -> p t n", t=16)
        mst = A(F0 + 6 * K, 16 * 128, BF16).rearrange("p (t n) -> p t n", t=16)
        nmst = A(F0 + 10 * K, 16 * 128, BF16).rearrange("p (t n) -> p t n", t=16)
        bct = A(F0 + 14 * K, 128, BF16)
        bsnt = A(F0 + 14 * K + 256, 128, BF16)
        assert F0 + 14 * K + 512 <= TOP
        sqf = A(0, 512, F32)
        ybf = [A(2 * K + i * K, 512, BF16) for i in range(2)]
        sqf2 = [A(4 * K + i * 2 * K, 512, F32) for i in range(2)]


        def f_stage0(n2):
            for gp in range(2):
                bank = gp
                for gg in range(2):
                    g = 2 * gp + gg
                    mm(ps[:, bank, gg * 256:(gg + 1) * 256], uT[:, g, n2:S:16], AB[:, g, :], True, True,
                       reads=[("uT", g, tb) for tb in range(4)] + ["AB"], writes=[("ps", bank)])
                src = ps[:, bank, :].rearrange("p (g s n) -> p s g n", g=2, s=2)
                if gp == 0:
                    act(PQ[:, n2, :, 0:2, :], src, AF.Copy, xreads=[("ps", bank)], writes=[("PQ", n2, 0)])
                else:
                    cpy("dve", PQ[:, n2, :, 2:4, :], src, xreads=[("ps", bank)], writes=[("PQ", n2, 1)])

        def f_stage1(n2):
            pr = [("PQ", n2, 0), ("PQ", n2, 1), ("ftab", 0), ("ftab", 1), ("ftab", 2)]
            br = 2 + 2 * (n2 % 2)
            Pall = PQ[:, n2, 0, :, :]
            Qall = PQ[:, n2, 1, :, :]
            mm(ps[:, br, :], mct[:, n2, :], Pall, True, False, reads=pr, writes=[("ps", br)])
            mm(ps[:, br, :], nmst[:, n2, :], Qall, False, True, reads=pr, writes=[("ps", br)])
            mm(ps[:, br + 1, :], mst[:, n2, :], Pall, True, False, reads=pr, writes=[("ps", br + 1)])
            mm(ps[:, br + 1, :], mct[:, n2, :], Qall, False, True, reads=pr, writes=[("ps", br + 1)])
            act(Yst[:, 0, :, n2, :], ps[:, br, :].rearrange("p (j e) -> p j e", e=8), AF.Copy,
                xreads=[("ps", br)], writes=[("Y", n2, 0)])
            cpy("dve", Yst[:, 1, :, n2, :], ps[:, br + 1, :].rearrange("p (j e) -> p j e", e=8),
                xreads=[("ps", br + 1)], writes=[("Y", n2, 1)])

        f_stage0(0)
        f_stage0(1)
        attn_out_b(prev)
        f_stage0(2)
        f_stage0(3)
        attn_out_c(prev)
        dump("ynaT", ynaT, [128, 4, S], BF16, [("ynaT", i) for i in range(NT)])
        att_last = [P.ops[e][-1] for e in ("pe", "act", "dve")]
        for k_, (dst_, src_) in enumerate(((mct, mc_d), (mst, ms_d), (nmst, nms_d), (bct, bc_d), (bsnt, bsn_d))):
            P.add("sp", (lambda o, i_: (lambda e: e.dma_start(out=o, in_=i_)))(dst_, src_), writes=[("ftab", k_)], dma=f"ftab{k_}",
                  extra=att_last)
        for n2 in range(NT):
            if n2 + 4 < NT:
                f_stage0(n2 + 4)
            f_stage1(n2)

        YALL = [("Y", n2, s2) for n2 in range(NT) for s2 in range(2)]

        def f_transp(jb):
            bank = 6 + jb % 2
            pst = ps[:, bank, :].bitcast(BF16)
            for jj in range(4):
                j = 4 * jb + jj
                for s2 in range(2):
                    o = (jj * 2 + s2) * 128
                    tr(pst[:, o:o + 128], Yst[:, s2, j, :, :].rearrange("p t e -> p (t e)"), reads=YALL, writes=[("ps", bank)])
            src = pst.rearrange("p (j s n) -> p j s n", j=4, s=2)
            if jb % 2 == 0:
                act(YT[:, 4 * jb:4 * jb + 4, :, :], src, AF.Copy, xreads=[("ps", bank)], writes=[("YT", jb)])
            else:
                cpy("dve", YT[:, 4 * jb:4 * jb + 4, :, :], src, xreads=[("ps", bank)], writes=[("YT", jb)])

        def f_stage2(jb):
            bank = jb % 2
            for jj in range(4):
                j = 4 * jb + jj
                o = jj * 128
                mm(ps[:, bank, o:o + 128], YT[:, j, 0, :], bct, True, False, reads=[("YT", jb), ("ftab", 3), ("ftab", 4)], writes=[("ps", bank)])
                mm(ps[:, bank, o:o + 128], YT[:, j, 1, :], bsnt, False, True, reads=[("YT", jb), ("ftab", 3), ("ftab", 4)], writes=[("ps", bank)])
            src = ps[:, bank, :].rearrange("p (j k e) -> p k j e", j=4, k=16)
            dst = yf[:, :, 32 * jb:32 * jb + 32].rearrange("p k (j e) -> p k j e", j=4)
            if jb % 2 == 0:
                cpy("dve", dst, src, xreads=[("ps", bank)], writes=[("yf", jb)])
            else:
                act(dst, src, AF.Copy, xreads=[("ps", bank)], writes=[("yf", jb)])

        f_transp(0)
        for jb in range(16):
            if jb + 1 < 16:
                f_transp(jb + 1)
            f_stage2(jb)

        YFALL = [("yf", jb) for jb in range(16)]

        def f_norm_a(k2):
            if k2 >= NT:
                return
            if k2 % 2 == 0:
                act(sqf2[0], yf[:, k2, :], AF.Square, reads=YFALL, writes=[("sqf", 0), ("ss", 4, k2)], accum=ss[:, 4, k2:k2 + 1])
            else:
                tt("pool", sqf2[1], yf[:, k2, :], yf[:, k2, :], ALU.mult, reads=YFALL, writes=[("sqf", 1)])
                P.add("dve", lambda e: e.reduce_sum(out=ss[:, 4, k2:k2 + 1], in_=sqf2[1], axis=mybir.AxisListType.X),
                      reads=[("sqf", 1)], writes=[("ss", 4, k2)])
            rstd_of(4, k2, 512)

        def f_norm_b(k2):
            b2 = k2 % 2
            tsc("dve", ybf[b2], yf[:, k2, :], rstd[:, 4, k2:k2 + 1], ALU.mult, reads=YFALL + [("rstd", 4, k2)], writes=[("ybf", b2)])
            bank = 2 + b2
            pst = ps[:, bank, :].bitcast(BF16)
            for c in range(4):
                tr(pst[:, c * 128:(c + 1) * 128], ybf[b2][:, c * 128:(c + 1) * 128], reads=[("ybf", b2)], writes=[("ps", bank)])
            act(yfT[:, :, k2 * 128:(k2 + 1) * 128], pst[:, 0:512].rearrange("p (c t) -> p c t", c=4), AF.Copy,
                xreads=[("ps", bank)], writes=[("yfT", k2)])

        SP2 = S + 2
        N0 = ((8 * SP2 * 2 + 255) // 256) * 256
        wo = A(N0 + 6 * K, 8 * 1024, BF16).rearrange("p (c n) -> p c n", c=8)
        wst = [A(Y0 + 32 * K + i * 4 * K, 1024, F32) for i in range(2)]
        f_last = [P.ops[e][-1] for e in ("pe", "act", "dve")]

        def wo_dma(ec):
            if ec >= 8:
                return
            b2 = ec % 2
            P.add("sp", (lambda o, i_: (lambda e: e.dma_start(out=o, in_=i_)))(wst[b2], w_out[ec * 128:(ec + 1) * 128, :]),
                  writes=[("wst", b2)], dma=f"wst{b2}", extra=f_last)

        def wo_scale(ec):
            b2 = ec % 2
            P.add("dve", (lambda o, i0, sc: (lambda e: e.scalar_tensor_tensor(out=o, in0=i0, scalar=sc, in1=gt1, op0=ALU.mult, op1=ALU.mult)))(
                wo[:, ec, :], wst[b2], gout_col[:, ec:ec + 1]),
                reads=[("wst", b2), "gout_col"] + MOD["gt1"], writes=[("wo", ec)], extra=f_last)

        wo_dma(0)
        wo_dma(1)
        f_norm_a(0)
        f_norm_a(1)
        for k2 in range(NT):
            f_norm_a(k2 + 2)
            f_norm_b(k2)
            if 1 <= k2 < 9:
                wo_scale(k2 - 1)
                wo_dma(k2 + 1)
        dump("yfT", yfT, [128, 4, S], BF16, [("yfT", k2) for k2 in range(NT)])
        P.barrier()

        hT2 = A(0, 8 * SP2, BF16).rearrange("p (c t) -> p c t", c=8)
        xt4 = [A(N0 + 22 * K + i * 4 * K, 1024, F32) for i in range(2)]
        assert N0 + 30 * K <= 64 * K
        x2 = A(64 * K, NT * 1024, F32).rearrange("p (t n) -> p t n", t=NT)
        t1_c = [A(TOP + i * 4 * K, 1024, F32) for i in range(2)]
        hb_c = [A(Y0 + 40 * K + i * 2 * K, 1024, BF16) for i in range(2)]
        junk_c = A(Y0 + 44 * K, 1024, BF16)
        assert Y0 + 46 * K <= TOP
        mset("pool", hT2[:, :, 0:1], 0.0, ["h2pad"])
        mset("pool", hT2[:, :, SP2 - 1:SP2], 0.0, ["h2pad"])
        WR = 3
        wup = [A(N0 + i * 2 * K, 8 * 128, BF16).rearrange("p (c n) -> p c n", c=8) for i in range(WR)]
        for n_, c0_ in ((0, 0), (1, DFF)):
            dma_pool(wup[n_], w_up[:, c0_:c0_ + 128].rearrange("(c p) n -> p c n", p=128), writes=[("wup", n_)], grp=f"wup{n_}")

        def p5_s1(i):
            norm_s1(i, x2[:, i, :], [("x2", i, 0), ("x2", i, 1)], 2, G2, MOD["G2"], sh2, MOD["sh2"], t1_c, hb_c, junk_c)

        dma_sp(xt4[0], x[0:128, :], writes=[("xt4", 0)], grp="xt0")
        for i in range(NT):
            b3 = i % 2
            if i + 1 < NT:
                dma_sp(xt4[(i + 1) % 2], x[(i + 1) * 128:(i + 2) * 128, :], writes=[("xt4", (i + 1) % 2)], grp=f"xt{(i + 1) % 2}")
            for dh in range(2):
                bank = (2 * i + dh) % 4
                ds = slice(dh * 512, (dh + 1) * 512)
                for ec in range(8):
                    src = yfT if ec < 4 else ynaT
                    mm(ps[:, bank, :], src[:, ec % 4, i * 128:(i + 1) * 128], wo[:, ec, ds], ec == 0, ec == 7,
                       reads=[("wo", ec)], writes=[("ps", bank)])
                tt("dve", x2[:, i, ds], ps[:, bank, :], xt4[b3][:, ds], ALU.add,
                   reads=[("xt4", b3)], xreads=[("ps", bank)], writes=[("x2", i, dh)])
            if i >= 2:
                norm_s2(i - 2, hT2, "hT2", 1, hb_c, (6, 7))
            p5_s1(i)
        norm_s2(NT - 2, hT2, "hT2", 1, hb_c, (6, 7))
        norm_s2(NT - 1, hT2, "hT2", 1, hb_c, (6, 7))
        dump("x2", x2, [128, NT, 1024], F32, [("x2", i, dh) for i in range(NT) for dh in range(2)])
        dump("hT2", hT2, [128, 8, SP2], BF16, [("hT2", i) for i in range(NT)] + ["h2pad"])
        p5_last = [P.ops[e][-1] for e in ("pe", "act", "dve", "pool")]
        wdst = [A(N0 + 6 * K + i * 4 * K, 1024, F32) for i in range(2)]
        AC0 = N0 + 14 * K
        accg = [A(AC0 + i * 4352, 1026, F32).rearrange("p (b n) -> p b n", b=3) for i in range(2)]
        accv = [A(AC0 + 8704 + i * 4352, 1026, F32).rearrange("p (b n) -> p b n", b=3) for i in range(2)]
        assert AC0 + 4 * 4352 <= 64 * K, AC0
        junk_d = A(AC0, 1024, BF16)
        aT = [A(128 * K + i * 16 * K, 4 * S, BF16).rearrange("p (c t) -> p c t", c=4) for i in range(2)]
        wd = [A(160 * K + i * 8 * K, 4 * 1024, BF16).rearrange("p (c n) -> p c n", c=4) for i in range(2)]
        stage = [A(TOP + i * 2 * K, 512, F32) for i in range(2)]
        sg = [A(TOP + 4 * K + i * 2176, 1026, BF16).rearrange("p (b n) -> p b n", b=3) for i in range(2)]
        gfin = A(TOP + 12 * K, 1024, F32)

        _psz = [4, 4, 4, 4, 3, 3]
        fpieces = [list(range(sum(_psz[:k]), sum(_psz[:k + 1]))) for k in range(len(_psz))]
        assert fpieces[-1][-1] == NFC - 1
        NP = len(fpieces)
        OFFS = [0, 341, 682]
        NCH = 2 * NFC
        ucnt = [0]
        ecnt = [0]
        ccnt = [0]
        scnt = [0]

        def chunk_col(n):
            return (n // 2) * 128 if n % 2 == 0 else DFF + (n // 2) * 128

        def wup_load(n):
            return

        def wup_cast(n):
            if n >= NCH:
                return
            c0 = chunk_col(n)
            dma_pool(wup[n % WR], w_up[:, c0:c0 + 128].rearrange("(c p) n -> p c n", p=128), writes=[("wup", n % WR)], grp=f"wup{n % WR}")


        def up_unit(wb, hf):
            u = ucnt[0] % 2
            ucnt[0] += 1
            rds = [("wup", wb), "h2pad"] + [("hT2", t) for t in range(8 * hf, 8 * hf + 8)] + [("hT2", 8 if hf == 0 else 7)]
            for b in range(3):
                bank = 3 * u + b
                c0 = hf * 1024 + OFFS[b]
                for kc in range(8):
                    mm(ps[:, bank, 0:344], wup[wb][:, kc, :], hT2[:, kc, c0:c0 + 344], kc == 0, kc == 7, reads=rds,
                       writes=[("psu", u), ("ps", bank)])
            return u

        def conv_unit(u, ch, acc, accname):
            pu = ps[:, 3 * u:3 * u + 3, :]
            ccnt[0] += 1
            P.add("act", (lambda o, i_, sc, bi: (lambda e: e.activation(out=o, in_=i_, func=AF.Identity, scale=sc, bias=bi)))(
                acc, pu[:, :, 1:343], convw[:, ch, 1:2], convb[:, ch:ch + 1]),
                reads=["convw", "convb"], xreads=[("psu", u)], writes=[accname], extra=(p5_last if ccnt[0] <= 4 else ()))
            stt("dve", acc, pu[:, :, 0:342], convw[:, ch, 0:1], acc, ALU.mult, ALU.add,
                reads=["convw", accname], xreads=[("psu", u)], writes=[accname])
            stt("dve", acc, pu[:, :, 2:344], convw[:, ch, 2:3], acc, ALU.mult, ALU.add,
                reads=["convw", accname], xreads=[("psu", u)], writes=[accname])

        pending = []

        def drain(k):
            for _ in range(k):
                if pending:
                    pending.pop(0)()

        def up_piece(pi, ndrain):
            ab = pi % 2
            for ci, j in enumerate(fpieces[pi]):
                wg = (2 * j) % WR
                wv = (2 * j + 1) % WR
                for hf in range(2):
                    k2 = hf
                    ug = up_unit(wg, hf)
                    conv_unit(ug, j, accg[k2], ("accg", k2))
                    wup_cast(2 * j + 2 + hf)
                    wup_load(2 * j + 4 + hf)
                    if ci == len(fpieces[pi]) - 1:
                        wd_dma(pi + 1, hf)
                    drain(ndrain)
                    uv = up_unit(wv, hf)
                    conv_unit(uv, NFC + j, accv[k2], ("accv", k2))
                    drain(ndrain)
                    act(sg[k2], accg[k2], AF.Silu, reads=[("accg", k2)], writes=[("sg", k2)])
                    base = hf * 1024
                    for b in range(3):
                        P.add("dve", (lambda o, i0, i1: (lambda e: e.tensor_tensor(out=o, in0=i0, in1=i1, op=ALU.mult)))(
                            aT[ab][:, ci, base + OFFS[b]:base + OFFS[b] + 342], sg[k2][:, b, :], accv[k2][:, b, :]),
                            reads=[("sg", k2), ("accv", k2)], writes=[("aT", ab, ci, hf)], extra=(p5_last if pi == 1 else ()))

        wd_pref = set()

        def wd_dma(pi, ci):
            if pi >= NP or ci >= len(fpieces[pi]) or (pi, ci) in wd_pref:
                return
            wd_pref.add((pi, ci))
            ex = p5_last if pi < 2 else ()
            j = fpieces[pi][ci]
            b2 = ci % 2
            P.add("sp", (lambda o, i_: (lambda e: e.dma_start(out=o, in_=i_)))(wdst[b2], w_down[j * 128:(j + 1) * 128, :]),
                  writes=[("wdst", b2)], dma=f"wdst{b2}", extra=ex)

        def load_wd(pi):
            ab = pi % 2
            ex = p5_last if pi < 2 else ()
            wd_dma(pi, 0)
            wd_dma(pi, 1)
            for ci, j in enumerate(fpieces[pi]):
                b2 = ci % 2
                P.add("pool", (lambda o, i0: (lambda e: e.tensor_tensor(out=o, in0=i0, in1=gt2, op=ALU.mult)))(wd[ab][:, ci, :], wdst[b2]),
                      reads=[("wdst", b2)] + MOD["gt2"], writes=[("wd", ab, ci)], extra=ex)
                wd_dma(pi, ci + 2)

        def down_group(pi, t, dh):
            ab = pi % 2
            n = len(fpieces[pi])
            bank = 6 + dh
            ds = slice(dh * 512, (dh + 1) * 512)
            for ci in range(n):
                mm(ps[:, bank, :], aT[ab][:, ci, t * 128:(t + 1) * 128], wd[ab][:, ci, ds], ci == 0, ci == n - 1,
                   reads=[("aT", ab, ci, t // 8), ("wd", ab, ci)], writes=[("ps", bank)])
            if dh == 1 and (pi != NP - 2 or t % 2 == 1):
                tt("dve", x2[:, t, ds], ps[:, bank, :], x2[:, t, ds], ALU.add,
                   reads=[("x2", t, dh)], xreads=[("ps", bank)], writes=[("x2", t, dh)])
            else:
                sb2 = scnt[0] % 2
                scnt[0] += 1
                act(stage[sb2], ps[:, bank, :], AF.Copy, xreads=[("ps", bank)], writes=[("stage", sb2)])
                tt("pool", x2[:, t, ds], x2[:, t, ds], stage[sb2], ALU.add,
                   reads=[("stage", sb2), ("x2", t, dh)], writes=[("x2", t, dh)])
            if pi == NP - 1 and dh == 1:
                if t > 0:
                    final_norm(t - 1)
                if t == NT - 1:
                    final_norm(t)

        def final_norm(t):
            xr = [("x2", t, 0), ("x2", t, 1)]
            act(junk_d, x2[:, t, :], AF.Square, reads=xr, writes=["junk", ("ss", 3, t)], accum=ss[:, 3, t:t + 1])
            rstd_of(3, t, D)
            stt("dve", x2[:, t, :], x2[:, t, :], rstd[:, 3, t:t + 1], gfin, ALU.mult, ALU.mult,
                reads=xr + [("rstd", 3, t), "gfin"], writes=xr)
            P.add("sp", lambda e: e.dma_start(out=y_out[t * 128:(t + 1) * 128, :], in_=x2[:, t, :]), reads=xr, dma="out")

        def queue_down(pi):
            for t in range(NT):
                for dh in range(2):
                    pending.append(lambda pi=pi, t=t, dh=dh: down_group(pi, t, dh))

        load_wd(0)
        P.add("sp", lambda e: e.dma_start(out=gfin, in_=g_fin_bc), writes=["gfin"], dma="gfin", extra=p5_last)
        up_piece(0, 0)
        for pi in range(1, NP):
            load_wd(pi)
            queue_down(pi - 1)
            nunits = 4 * len(fpieces[pi])
            up_piece(pi, -(-32 // nunits))
            drain(len(pending))
        queue_down(NP - 1)
        drain(len(pending))

        P.emit(nc, final_dma_groups=["out"])
    return nc, dbg


_NC_CACHE = None


def kernel(**inputs):
    global _NC_CACHE
    inp = {k: np.asarray(v) for k, v in inputs.items()}
    maps = _layout_inputs(inp)
    if _NC_CACHE is None:
        _NC_CACHE = build_nc()
    nc, dbg = _NC_CACHE
    res = run_bass_kernel_spmd(nc, maps, core_ids=list(range(8)))
    out = np.stack([np.asarray(res.results[b]["y"], dtype=np.float32) for b in range(8)], axis=0)
    if DEBUG:
        kernel.debug = [{k: np.asarray(res.results[b]["dbg_" + k]) for k in dbg} for b in range(8)]
    return out
```

```python
import math
from contextlib import ExitStack

import numpy as np
import ml_dtypes
import concourse.bass as bass
import concourse.mybir as mybir
from concourse.bass_utils import run_bass_kernel_spmd

F32 = mybir.dt.float32
BF16 = mybir.dt.bfloat16
AF = mybir.ActivationFunctionType
ALU = mybir.AluOpType

D = 1024
S = 2048
NT = 16
DFF = 2816
NFC = 22
EPS = 1e-6
ENGS = ["pe", "act", "dve", "pool", "sp"]
DEBUG = []


class Op:
    __slots__ = ("eng", "fn", "deps", "signal", "seq", "dma", "twrites")

    def __init__(self, eng, fn, dma):
        self.eng = eng
        self.fn = fn
        self.deps = []
        self.signal = False
        self.seq = 0
        self.dma = dma
        self.twrites = ()


class Prog:
    def __init__(self):
        self.ops = {e: [] for e in ENGS}
        self.last_w = {}
        self.readers = {}
        self.dma_groups = {}
        self.bar = None
        self.bar_done = set()

    def barrier(self):
        lasts = []
        for e in ENGS:
            for op in reversed(self.ops[e]):
                if op.dma is None:
                    lasts.append(op)
                    break
        seen = set()
        for e in ENGS:
            for op in reversed(self.ops[e]):
                if op.dma is not None and op.dma not in seen:
                    seen.add(op.dma)
                    lasts.append(op)
        self.bar = lasts
        self.bar_done = set()

    def add(self, eng, fn, reads=(), writes=(), xreads=(), dma=None, extra=()):
        op = Op(eng, fn, dma)
        op.twrites = tuple(writes)
        deps = {}

        def consider(d, raw):
            if d is None or d is op:
                return
            if d.eng == eng and d.dma is None and not raw:
                return
            deps[id(d)] = d

        for r in reads:
            consider(self.last_w.get(r), True)
        for r in xreads:
            w = self.last_w.get(r)
            consider(w, w is not None and r in w.twrites)
            for rd in self.readers.get(r, ()):
                consider(rd, False)
        for r in writes:
            consider(self.last_w.get(r), False)
            for rd in self.readers.get(r, ()):
                consider(rd, False)
        for d in extra:
            consider(d, True)
        if self.bar is not None and eng not in self.bar_done:
            self.bar_done.add(eng)
            for d in self.bar:
                consider(d, True)
        op.deps = [(d, self.dma_groups[d.dma]["count"] if d.dma is not None else 0) for d in deps.values()]
        for d, _ in op.deps:
            d.signal = True
        for r in reads:
            self.readers.setdefault(r, []).append(op)
        for r in list(writes) + list(xreads):
            self.last_w[r] = op
            self.readers[r] = []
        if dma is not None:
            g = self.dma_groups.setdefault(dma, {"eng": eng, "count": 0})
            assert g["eng"] == eng, (dma, g["eng"], eng)
            g["count"] += 1
        self.ops[eng].append(op)
        return op

    def emit(self, nc, final_dma_groups):
        with ExitStack() as st:
            esem = {e: st.enter_context(nc.semaphore("s_" + e)) for e in ENGS}
            dsem = {g: st.enter_context(nc.semaphore("d_" + g)) for g in self.dma_groups}
            for e in ENGS:
                c = 0
                for op in self.ops[e]:
                    if op.dma is None and op.signal:
                        c += 1
                        op.seq = c
            block = st.enter_context(nc.Block())

            def run(e, eng):
                known = {}
                for op in self.ops[e]:
                    need = {}
                    for d, cnt in op.deps:
                        if d.dma is not None:
                            k = ("d", d.dma)
                            v = 16 * cnt
                        else:
                            k = ("e", d.eng)
                            v = d.seq
                        if v > need.get(k, 0):
                            need[k] = v
                    for k, v in need.items():
                        if known.get(k, 0) >= v:
                            continue
                        known[k] = v
                        eng.wait_ge(dsem[k[1]] if k[0] == "d" else esem[k[1]], v)
                    ins = op.fn(eng)
                    if op.dma is not None:
                        ins.then_inc(dsem[op.dma], 16)
                    elif op.signal:
                        ins.then_inc(esem[e], 1)
                if e == "sp":
                    for g in final_dma_groups:
                        eng.wait_ge(dsem[g], 16 * self.dma_groups[g]["count"])

            @block.tensor
            def _(eng):
                run("pe", eng)

            @block.scalar
            def _(eng):
                run("act", eng)

            @block.vector
            def _(eng):
                run("dve", eng)

            @block.gpsimd
            def _(eng):
                run("pool", eng)

            @block.sync
            def _(eng):
                run("sp", eng)


def _attn_variant(i):
    def rs(r):
        return min(max(r - 4, 0), 24)
    r0 = 2 * i
    jb = rs(r0) // 2
    p = np.arange(128)
    half = p // 64
    kc = p % 64
    qc = np.arange(64)
    cstart = np.clip(qc - 8, 0, 48)
    vcol = (kc[:, None] >= cstart[None, :]) & (kc[:, None] < cstart[None, :] + 16)
    dc = np.clip(kc[:, None] - qc[None, :], -15, 15) + 15
    drA = np.zeros((128, 4, 2, 64), np.int64)
    mA = np.zeros((128, 4, 2, 64), bool)
    for c in range(4):
        kr = 2 * (jb + c) + half
        for qh in range(2):
            r = r0 + qh
            vrow = (kr >= rs(r)) & (kr <= rs(r) + 7)
            drA[:, c, qh, :] = np.clip(kr - r, -7, 7)[:, None] + 7
            mA[:, c, qh, :] = vrow[:, None] & vcol
    dcA = np.broadcast_to(dc[:, None, None, :], (128, 4, 2, 64))
    extra = rs(r0 + 1) % 2 == 1
    drB = mB = None
    if extra:
        kr = 2 * (jb + 4) + half
        r = r0 + 1
        vrow = (kr >= rs(r)) & (kr <= rs(r) + 7)
        drB = np.broadcast_to(np.clip(kr - r, -7, 7)[:, None] + 7, (128, 64))
        mB = vrow[:, None] & vcol
    return jb, extra, drA, dcA, mA, drB, dc, mB


VARIANT_OF_TILE = {0: 1, 1: 2, 14: 3, 15: 4}


def _host_constants():
    c = {}
    c["ident"] = np.eye(128, dtype=ml_dtypes.bfloat16)
    n = np.arange(128)
    ang = 2.0 * np.pi * ((n[:, None] * n[None, :]) % 128) / 128.0
    c["cc"] = (np.cos(ang) / 512.0).astype(ml_dtypes.bfloat16)
    c["sc"] = (np.sin(ang) / 512.0).astype(ml_dtypes.bfloat16)
    n1 = np.arange(128)
    n2 = np.arange(16)
    k1 = np.arange(128)
    tt = 16 * n1[:, None] + n2[None, :]
    a1 = 2.0 * np.pi * ((tt[:, :, None] * k1[None, None, :]) % S) / S
    c["mc"] = np.cos(a1).astype(ml_dtypes.bfloat16)
    c["ms"] = np.sin(a1).astype(ml_dtypes.bfloat16)
    c["nms"] = (-np.sin(a1)).astype(ml_dtypes.bfloat16)
    a2 = 2.0 * np.pi * np.outer(np.arange(16), np.arange(16)) / 16.0
    eye8 = np.eye(8)
    c["bc"] = np.kron(np.cos(a2), eye8).astype(ml_dtypes.bfloat16)
    c["bsn"] = np.kron(-np.sin(a2), eye8).astype(ml_dtypes.bfloat16)
    c["ones"] = np.ones((128, 128), dtype=ml_dtypes.bfloat16)
    return c


_CONST = None


def _layout_inputs(inp):
    global _CONST
    if _CONST is None:
        _CONST = _host_constants()
    f32 = np.float32

    def bc(v):
        return np.ascontiguousarray(np.broadcast_to(np.asarray(v, f32)[None, :], (128, v.shape[0])))

    def col(v, nch):
        return np.ascontiguousarray(np.asarray(v, f32).reshape(nch, 128).T)

    rpb = np.asarray(inp["rpb"][0], f32)
    biasA = np.zeros((5, 128, 8, 512), f32)
    maskA = np.zeros((5, 128, 512), f32)
    biasB = np.zeros((128, 8, 64), f32)
    maskB = np.zeros((128, 64), f32)
    for i, v in [(5, 0), (0, 1), (1, 2), (14, 3), (15, 4)]:
        jb, extra, drA, dcA, mA, drB, dcB, mB = _attn_variant(i)
        g = rpb[:, drA, dcA]
        biasA[v] = g.reshape(8, 128, 512).transpose(1, 0, 2)
        maskA[v] = mA.reshape(128, 512).astype(f32)
        if v == 0:
            gb = rpb[:, drB, dcB]
            biasB[:] = gb.transpose(1, 0, 2)
            maskB[:] = mB.astype(f32)
    shared = dict(_CONST)
    shared.update({
        "w_ada": np.ascontiguousarray(inp["w_ada"][0], f32),
        "b_ada_bc": bc(inp["b_ada"][0]),
        "g_mix_bc": bc(inp["g_mix"][0]),
        "g_ffn_bc": bc(inp["g_ffn"][0]),
        "g_fin_bc": bc(inp["g_final"]),
        "w_in": np.ascontiguousarray(inp["w_in"][0], f32),
        "w_four": np.ascontiguousarray(inp["w_four"][0], f32),
        "w_out": np.ascontiguousarray(inp["w_out"][0], f32),
        "w_up": np.ascontiguousarray(inp["w_up"][0], f32),
        "w_down": np.ascontiguousarray(inp["w_down"][0], f32),
        "gout_col": np.ascontiguousarray(np.concatenate([col(inp["g_four_out"][0], 4), col(inp["g_na_out"][0], 4)], axis=1)),
        "convw_col": np.ascontiguousarray(np.asarray(inp["conv_w"][0], f32).reshape(3, 44, 128).transpose(2, 1, 0)),
        "convb_col": col(inp["conv_b"][0], 44),
        "biasA": biasA, "maskA": maskA, "biasB": biasB, "maskB": maskB,
    })
    maps = []
    for b in range(8):
        m = dict(shared)
        m["x"] = np.ascontiguousarray(inp["x"][b], f32)
        m["ccol"] = col(inp["c"][b], 8)
        maps.append(m)
    return maps


def build_nc():
    nc = bass.Bass("TRN2", target_bir_lowering=False)

    def din(name, shape, dt=F32):
        return nc.dram_tensor(name, shape, dt, kind="ExternalInput").ap()

    x = din("x", [S, D])
    ccol = din("ccol", [128, 8])
    w_ada = din("w_ada", [D, 6 * D])
    b_ada_bc = din("b_ada_bc", [128, 6 * D])
    g_mix_bc = din("g_mix_bc", [128, D])
    g_ffn_bc = din("g_ffn_bc", [128, D])
    g_fin_bc = din("g_fin_bc", [128, D])
    w_in = din("w_in", [D, 2048])
    w_four = din("w_four", [4, 128, 128])
    w_out = din("w_out", [D, D])
    w_up = din("w_up", [D, 2 * DFF])
    w_down = din("w_down", [DFF, D])
    gout_col_d = din("gout_col", [128, 8])
    convw_d = din("convw_col", [128, 44, 3])
    convb_d = din("convb_col", [128, 44])
    biasA_d = din("biasA", [5, 128, 8, 512])
    maskA_d = din("maskA", [5, 128, 512])
    biasB_d = din("biasB", [128, 8, 64])
    maskB_d = din("maskB", [128, 64])
    ident_d = din("ident", [128, 128], BF16)
    cc_d = din("cc", [128, 128], BF16)
    sc_d = din("sc", [128, 128], BF16)
    mc_d = din("mc", [128, 16, 128], BF16)
    ms_d = din("ms", [128, 16, 128], BF16)
    nms_d = din("nms", [128, 16, 128], BF16)
    bc_d = din("bc", [128, 128], BF16)
    bsn_d = din("bsn", [128, 128], BF16)
    ones_d = din("ones", [128, 128], BF16)
    y_out = nc.dram_tensor("y", [S, D], F32, kind="ExternalOutput").ap()
    dbg = {}

    P = Prog()
    with ExitStack() as st:
        ARENA_F32 = 51712
        arena = st.enter_context(nc.sbuf_tensor("arena", [128, ARENA_F32], F32))
        ps = st.enter_context(nc.psum_tensor("ps", [128, 8, 512], F32))

        def A(off, n_elem, dt):
            assert off % 4 == 0
            nb = n_elem * (2 if dt == BF16 else 4)
            assert nb % 4 == 0 and off + nb <= ARENA_F32 * 4, (off, nb)
            v = arena[:, off // 4:(off + nb) // 4]
            return v.bitcast(BF16) if dt == BF16 else v

        def sb(name, shape, dt):
            return st.enter_context(nc.sbuf_tensor(name, shape, dt))

        K = 1024
        ident = sb("ident_sb", [128, 128], BF16)
        ones = sb("ones_sb", [128, 128], BF16)
        c_sb = sb("c_sb", [128, 8], F32)
        epsb = sb("epsb", [128, 1], F32)
        ss = sb("ss", [128, 5, NT], F32)
        rstd = sb("rstd", [128, 5, NT], F32)
        gout_col = sb("gout_col_sb", [128, 8], F32)
        convw = sb("convw_sb", [128, 44, 3], F32)
        convb = sb("convb_sb", [128, 44], F32)
        rden = sb("rden", [128, 2, 8], F32)

        TOP = 176 * K
        G1 = A(TOP + 0 * K, 1024, F32)
        sh1 = A(TOP + 4 * K, 1024, F32)
        gt1 = A(TOP + 8 * K, 1024, F32)
        G2 = A(TOP + 12 * K, 1024, F32)
        sh2 = A(TOP + 16 * K, 1024, F32)
        gt2 = A(TOP + 22 * K, 1024, F32)

        def dma_sp(out, in_, writes=(), reads=(), grp="c0"):
            return P.add("sp", lambda e: e.dma_start(out=out, in_=in_), reads=reads, writes=writes, dma=grp)

        def dma_pool(out, in_, writes=(), reads=(), grp="p0"):
            return P.add("pool", lambda e: e.dma_start(out=out, in_=in_), reads=reads, writes=writes, dma=grp)

        def mm(out, lhsT, rhs, start, stop, reads, writes):
            return P.add("pe", lambda e: e.matmul(out, lhsT=lhsT, rhs=rhs, start=start, stop=stop), reads=reads, writes=writes)

        def tr(out, in_, reads, writes):
            return P.add("pe", lambda e: e.transpose(out=out, in_=in_, identity=ident[:]), reads=list(reads) + ["ident"], writes=writes)

        def act(out, in_, func, reads=(), writes=(), xreads=(), scale=None, bias=None, accum=None):
            kw = {}
            if scale is not None:
                kw["scale"] = scale
            if bias is not None:
                kw["bias"] = bias
            if accum is not None:
                kw["accum_out"] = accum
            return P.add("act", lambda e: e.activation(out=out, in_=in_, func=func, **kw), reads=reads, writes=writes, xreads=xreads)

        def tt(eng, out, in0, in1, op, reads=(), writes=(), xreads=()):
            return P.add(eng, lambda e: e.tensor_tensor(out=out, in0=in0, in1=in1, op=op), reads=reads, writes=writes, xreads=xreads)

        def stt(eng, out, in0, scalar, in1, op0, op1, reads=(), writes=(), xreads=()):
            return P.add(eng, lambda e: e.scalar_tensor_tensor(out=out, in0=in0, scalar=scalar, in1=in1, op0=op0, op1=op1),
                         reads=reads, writes=writes, xreads=xreads)

        def tsc(eng, out, in0, s1, op0, reads=(), writes=(), xreads=()):
            return P.add(eng, lambda e: e.tensor_scalar(out=out, in0=in0, scalar1=s1, scalar2=None, op0=op0),
                         reads=reads, writes=writes, xreads=xreads)

        def recip(out, in_, reads=(), writes=(), xreads=()):
            return P.add("dve", lambda e: e.reciprocal(out=out, in_=in_), reads=reads, writes=writes, xreads=xreads)

        def cpy(eng, out, in_, reads=(), writes=(), xreads=()):
            return P.add(eng, lambda e: e.tensor_copy(out=out, in_=in_), reads=reads, writes=writes, xreads=xreads)

        def mset(eng, ap, val, writes):
            return P.add(eng, lambda e: e.memset(ap, val), writes=writes)

        def dump(name, ap_sb, shape, dt, reads):
            if name not in DEBUG:
                return
            t = nc.dram_tensor("dbg_" + name, shape, dt, kind="ExternalOutput").ap()
            dbg[name] = t
            P.add("sp", lambda e: e.dma_start(out=t, in_=ap_sb), reads=reads, dma="out")

        def rstd_of(sidx, i, n):
            act(rstd[:, sidx, i:i + 1], ss[:, sidx, i:i + 1], AF.Sqrt, reads=[("ss", sidx, i), "epsb"], writes=[("rstd", sidx, i)],
                scale=1.0 / n, bias=epsb[:, 0:1])
            recip(rstd[:, sidx, i:i + 1], rstd[:, sidx, i:i + 1], reads=[("rstd", sidx, i)], writes=[("rstd", sidx, i)])

        cs_rep = sb("cs_rep", [128, 8, 128], BF16)

        dma_sp(ident[:], ident_d, writes=["ident"])
        dma_sp(c_sb[:], ccol, writes=["c_sb"])
        dma_sp(gout_col[:], gout_col_d, writes=["gout_col"])
        dma_sp(convw[:], convw_d, writes=["convw"])
        dma_sp(convb[:], convb_d, writes=["convb"])
        mset("dve", epsb[:], EPS, ["epsb"])
        act(cs_rep[:], c_sb[:].unsqueeze(2).broadcast_to([128, 8, 128]), AF.Silu, reads=["c_sb"], writes=["cs_rep"])
        mpieces = [(1, G1, g_mix_bc), (0, sh1, None), (2, gt1, None), (4, G2, g_ffn_bc), (3, sh2, None), (5, gt2, None)]
        mcast = [0]

        def mod_piece(pi, w32, wbf, bada_b, gtmp_b, tag, banks, ncols=1024, dq=None):
            blk, dst, gsrc = mpieces[pi]
            dq = dq or dma_sp
            dq(bada_b, b_ada_bc[:, blk * 1024:(blk + 1) * 1024], writes=[("bada", tag)], grp=f"bada{tag}")
            if gsrc is not None:
                dq(gtmp_b, gsrc, writes=[("gtmp", tag)], grp=f"bada{tag}")
            for rnd in range(1024 // ncols):
                c0 = blk * 1024 + rnd * ncols
                for kc in range(8):
                    dq(w32[:, kc, :], w_ada[kc * 128:(kc + 1) * 128, c0:c0 + ncols],
                       writes=[("w32", tag, kc)], grp=f"w32{tag}_{kc}")
                for kc in range(8):
                    eng = "dve" if mcast[0] % 3 != 2 else "act"
                    mcast[0] += 1
                    if eng == "dve":
                        cpy("dve", wbf[:, kc, :], w32[:, kc, :], reads=[("w32", tag, kc)], writes=[("wada", tag, kc)])
                    else:
                        act(wbf[:, kc, :], w32[:, kc, :], AF.Copy, reads=[("w32", tag, kc)], writes=[("wada", tag, kc)])
                for h2 in range(ncols // 512):
                    hh = rnd * (ncols // 512) + h2
                    bank = banks[hh]
                    hs = slice(hh * 512, (hh + 1) * 512)
                    for kc in range(8):
                        mm(ps[:, bank, :], cs_rep[:, kc, :], wbf[:, kc, h2 * 512:(h2 + 1) * 512], kc == 0, kc == 7,
                           reads=["cs_rep", ("wada", tag, kc)], writes=[("ps", bank)])
                    dname = ("mod", pi, hh)
                    tt("dve", dst[:, hs], ps[:, bank, :], bada_b[:, hs], ALU.add,
                       reads=[("bada", tag)], xreads=[("ps", bank)], writes=[dname])
                    if gsrc is not None:
                        stt("dve", dst[:, hs], dst[:, hs], 1.0, gtmp_b[:, hs], ALU.add, ALU.mult,
                            reads=[dname, ("gtmp", tag)], writes=[dname])

        w32a = [A(64 * K + i * 32 * K, 8 * 1024, F32).rearrange("p (c n) -> p c n", c=8) for i in range(2)]
        wbfa = [A(128 * K + i * 16 * K, 8 * 1024, BF16).rearrange("p (c n) -> p c n", c=8) for i in range(2)]
        badaa = [A(160 * K + i * 4 * K, 1024, F32) for i in range(2)]
        gtmpa = [A(168 * K + i * 4 * K, 1024, F32) for i in range(2)]
        for pi in range(2):
            b = pi % 2
            mod_piece(pi, w32a[b], wbfa[b], badaa[b], gtmpa[b], b, (2 * b, 2 * b + 1))
        win = A(32 * K, 8 * 2048, BF16).rearrange("p (c n) -> p c n", c=8)
        wi32 = [A(i * 8 * K, 2048, F32) for i in range(4)]
        for kc in range(8):
            b4 = kc % 4
            dma_sp(wi32[b4], w_in[kc * 128:(kc + 1) * 128, :], writes=[("wi32", b4)], grp=f"wi32{b4}")
            cpy("dve", win[:, kc, :], wi32[b4], reads=[("wi32", b4)], writes=[("win", kc)])
        MOD = {name: [("mod", pi, 0), ("mod", pi, 1)] for pi, name in enumerate(["G1", "sh1", "gt1", "G2", "sh2", "gt2"])}
        P.barrier()

        hT = A(0, 8 * S, BF16).rearrange("p (c t) -> p c t", c=8)
        qT = A(64 * K, 4 * S, BF16).rearrange("p (c t) -> p c t", c=4)
        kT = A(80 * K, 4 * S, BF16).rearrange("p (c t) -> p c t", c=4)
        VO = 96 * K
        Vext = A(VO, NT * 8 * 65, BF16).rearrange("p (t h d) -> p t h d", t=NT, h=8)
        UO = VO + 16640 + 256
        uT = A(UO, 4 * S, BF16).rearrange("p (c t) -> p c t", c=4)
        XO = UO + 16 * K
        xt = [A(XO + i * 4 * K, 1024, F32) for i in range(3)]
        t1_a = [A(XO + 12 * K + i * 4 * K, 1024, F32) for i in range(2)]
        hb_a = [A(XO + 20 * K + i * 2 * K, 1024, BF16) for i in range(2)]
        junk_a = A(XO + 24 * K, 1024, BF16)
        assert XO + 26 * K <= TOP

        mset("dve", Vext[:, :, :, 64:65], 1.0, ["vones"])

        def norm_s1(i, src_ap, src_res, sidx, Gt, Gres, sht, shres, t1, hb, junk, pool_ok=True):
            b2 = i % 2
            act(junk, src_ap, AF.Square, reads=src_res, writes=["junk", ("ss", sidx, i)], accum=ss[:, sidx, i:i + 1])
            rstd_of(sidx, i, D)
            stt("dve", t1[b2], src_ap, rstd[:, sidx, i:i + 1], Gt, ALU.mult, ALU.mult,
                reads=list(src_res) + [("rstd", sidx, i)] + Gres, writes=[("t1", b2)])
            tt("pool" if (pool_ok and i % 2 == 0) else "dve", hb[b2], t1[b2], sht, ALU.add,
               reads=[("t1", b2)] + shres, writes=[("hb", b2)])

        def norm_s2(i, dstT, dst_res, col0, hb, tbanks):
            b2 = i % 2
            bank = tbanks[b2]
            pst = ps[:, bank, :].bitcast(BF16)
            for c in range(8):
                tr(pst[:, c * 128:(c + 1) * 128], hb[b2][:, c * 128:(c + 1) * 128], reads=[("hb", b2)], writes=[("ps", bank)])
            act(dstT[:, :, col0 + i * 128:col0 + (i + 1) * 128], pst.rearrange("p (c t) -> p c t", c=8), AF.Copy,
                xreads=[("ps", bank)], writes=[(dst_res, i)])

        pbank = [0]

        def next_bank():
            b = pbank[0]
            pbank[0] = b + 1 if b < 4 else 0
            return b

        def proj_groups(tb):
            ts_ = slice(tb * 512, (tb + 1) * 512)
            hres = [("hT", 4 * tb + j) for j in range(4)]
            out = []

            def fm(col_base, dst, dst_name, scale, fc):
                bank = next_bank()
                for kc in range(8):
                    mm(ps[:, bank, :], win[:, kc, col_base + fc * 128:col_base + (fc + 1) * 128], hT[:, kc, ts_], kc == 0, kc == 7,
                       reads=[("win", kc)] + hres, writes=[("ps", bank)])
                if scale is None:
                    cpy("dve", dst[:, fc, ts_], ps[:, bank, :], xreads=[("ps", bank)], writes=[(dst_name, fc, tb)])
                else:
                    act(dst[:, fc, ts_], ps[:, bank, :], AF.Copy, xreads=[("ps", bank)], writes=[(dst_name, fc, tb)], scale=scale)

            def vm(tt_):
                bank = next_bank()
                for kc in range(8):
                    mm(ps[:, bank, :], hT[:, kc, tt_ * 128:(tt_ + 1) * 128], win[:, kc, 1536:2048], kc == 0, kc == 7,
                       reads=[("win", kc), ("hT", tt_)], writes=[("ps", bank)])
                act(Vext[:, tt_, :, 0:64], ps[:, bank, :].rearrange("p (h d) -> p h d", h=8), AF.Copy,
                    xreads=[("ps", bank)], writes=[("V", tt_)])

            for col_base, dst, dst_name, scale in ((0, uT, "uT", None), (512, qT, "qT", 0.125), (1024, kT, "kT", None)):
                for fc in range(4):
                    out.append(lambda cb=col_base, d=dst, dn=dst_name, sc=scale, fc=fc: fm(cb, d, dn, sc, fc))
            for tt_ in range(4 * tb, 4 * tb + 4):
                out.append(lambda tt_=tt_: vm(tt_))
            return out

        def p1_s1(i):
            if i >= NT:
                return
            b3 = i % 3
            dma_sp(xt[b3], x[i * 128:(i + 1) * 128, :], writes=[("xt", b3)], grp=f"xt{b3}")
            norm_s1(i, xt[b3], [("xt", b3)], 0, G1, MOD["G1"], sh1, MOD["sh1"], t1_a, hb_a, junk_a, pool_ok=False)

        LS0 = XO + 26 * K
        lwb = [A(LS0 + b * 8 * K, 8 * 512, BF16).rearrange("p (c n) -> p c n", c=8) for b in range(2)]
        lgt = A(LS0 + 16 * K, 1024, F32)
        assert LS0 + 20 * K <= TOP
        LGT = ["lgt", "lgt"]
        lrounds = [(pi, r) for pi in range(2, 6) for r in range(2)]

        def late_dma(k):
            pi, r = lrounds[k]
            blk, dst, gsrc = mpieces[pi]
            if r == 0:
                dma_sp(dst, b_ada_bc[:, blk * 1024:(blk + 1) * 1024], writes=[("mod", pi, 0), ("mod", pi, 1)], grp=f"lb{pi}")
                if gsrc is not None:
                    dma_sp(lgt, gsrc, writes=["lgt"], grp=f"lb{pi}")
            c0 = blk * 1024 + r * 512
            for kc in range(8):
                dma_pool(lwb[k % 2][:, kc, :], w_ada[kc * 128:(kc + 1) * 128, c0:c0 + 512],
                         writes=[("lwb", k % 2, kc)], grp=f"lw{k % 2}_{kc}")

        def late_round(k):
            pi, r = lrounds[k]
            blk, dst, gsrc = mpieces[pi]
            hs = slice(r * 512, (r + 1) * 512)
            for kc in range(8):
                mm(ps[:, 5, :], cs_rep[:, kc, :], lwb[k % 2][:, kc, :], kc == 0, kc == 7,
                   reads=["cs_rep", ("lwb", k % 2, kc)], writes=[("ps", 5)])
            tt("dve", dst[:, hs], ps[:, 5, :], dst[:, hs], ALU.add, reads=[("mod", pi, r)], xreads=[("ps", 5)], writes=[("mod", pi, r)])
            if gsrc is not None:
                stt("dve", dst[:, hs], dst[:, hs], 1.0, lgt[:, hs], ALU.add, ALU.mult, reads=[("mod", pi, r), LGT[r]], writes=[("mod", pi, r)])

        def late_hook(j):
            if j % 2 == 1 and j >= 3:
                late_round((j - 3) // 2)
            if j % 2 == 0:
                late_dma(j // 2)

        p1_s1(0)
        p1_s1(1)
        for i in range(4):
            norm_s2(i, hT, "hT", 0, hb_a, (6, 7))
            p1_s1(i + 2)
            late_hook(i)
        for tb in range(4):
            groups = proj_groups(tb)
            for gi, g in enumerate(groups):
                g()
                if gi % 4 == 3 and tb < 3:
                    i = 4 * (tb + 1) + gi // 4
                    norm_s2(i, hT, "hT", 0, hb_a, (6, 7))
                    p1_s1(i + 2)
                    late_hook(i)
        late_round(7)
        dump("hT", hT, [128, 8, S], BF16, [("hT", i) for i in range(NT)])
        dump("qT", qT, [128, 4, S], BF16, [("qT", fc, tb) for fc in range(4) for tb in range(4)])
        dump("kT", kT, [128, 4, S], BF16, [("kT", fc, tb) for fc in range(4) for tb in range(4)])
        dump("uT", uT, [128, 4, S], BF16, [("uT", fc, tb) for fc in range(4) for tb in range(4)])
        dump("V", Vext, [128, NT, 8, 65], BF16, [("V", t) for t in range(NT)] + ["vones"])
        P.barrier()

        EA = A(0, 5 * 8 * 512, BF16).rearrange("p (v h n) -> p v h n", v=5, h=8)
        EB = A(40 * K, 8 * 64, BF16).rearrange("p (h n) -> p h n", h=8)
        bstage = A(41 * K, 8 * 512, F32).rearrange("p (h n) -> p h n", h=8)
        mstage = A(57 * K, 512, F32)
        Y0 = UO + 16 * K
        ynaT = A(Y0, 4 * S, BF16).rearrange("p (c t) -> p c t", c=4)
        yfT = A(Y0 + 16 * K, 4 * S, BF16).rearrange("p (c t) -> p c t", c=4)
        o_n = [A(Y0 + 32 * K + i * 2 * K, 512, F32) for i in range(2)]
        o_bf = [A(Y0 + 36 * K + i * K, 512, BF16) for i in range(2)]
        junk_b = A(Y0 + 38 * K, 512, BF16)
        PB0 = Y0 + 20 * K
        peA = [A(PB0 + i * 2 * K, 1024, BF16).rearrange("p (e n) -> p e n", e=2) for i in range(2)]
        pTA = [A(PB0 + 4 * K + i * 2 * K, 1024, BF16).rearrange("p (e n) -> p e n", e=2) for i in range(2)]
        peB = [A(PB0 + 8 * K + i * 256, 128, BF16).rearrange("p (e n) -> p e n", e=2) for i in range(2)]
        pTB = [A(PB0 + 8 * K + 512 + i * 256, 128, BF16).rearrange("p (e n) -> p e n", e=2) for i in range(2)]
        bstB = A(Y0 + 16 * K, 8 * 64, F32).rearrange("p (h n) -> p h n", h=8)
        mstB = A(Y0 + 18 * K, 64, F32)
        assert Y0 + 39 * K <= TOP

        bsth = [bstage[:, 0:4, :], bstage[:, 4:8, :]]
        ecount = [0]

        def load_E(v):
            dma_sp(mstage, maskA_d[v], writes=["mstage"], grp="mst")
            for hh in range(2):
                dma_sp(bsth[hh], biasA_d[v][:, hh * 4:(hh + 1) * 4, :],
                       writes=[("bstage", hh), ("bsq", 2 * hh), ("bsq", 2 * hh + 1)], grp=f"bst{hh}")

        def build_E0_quarters():
            dma_sp(mstage, maskA_d[0], writes=["mstage"], grp="mst")
            for q in range(4):
                dma_sp(bstage[:, 2 * q:2 * q + 2, :], biasA_d[0][:, 2 * q:2 * q + 2, :], writes=[("bsq", q)], grp=f"bsq{q}")
            dma_sp(bstB, biasB_d, writes=["bstB"], grp="mstB")
            dma_sp(mstB, maskB_d, writes=["mstB"], grp="mstB")
            for q in range(4):
                act(bstage[:, 2 * q:2 * q + 2, :], bstage[:, 2 * q:2 * q + 2, :], AF.Exp, reads=[("bsq", q)], writes=[("bsq", q)])
                tt("dve", EA[:, 0, 2 * q:2 * q + 2, :], bstage[:, 2 * q:2 * q + 2, :], mstage.unsqueeze(1).broadcast_to([128, 2, 512]),
                   ALU.mult, reads=[("bsq", q), "mstage"], writes=[("EA", 0, q)])
                if q == 0:
                    act(bstB, bstB, AF.Exp, reads=["bstB"], writes=["bstB"])
                    tt("dve", EB, bstB, mstB.unsqueeze(1).broadcast_to([128, 8, 64]), ALU.mult,
                       reads=["bstB", "mstB"], writes=["EB"])

        def build_E_quarter(v, q):
            hh = q // 2
            sl = bstage[:, 2 * q:2 * q + 2, :]
            act(sl, sl, AF.Exp, reads=[("bstage", hh)], writes=[("bstage", hh)])
            tt("dve", EA[:, v, 2 * q:2 * q + 2, :], sl, mstage.unsqueeze(1).broadcast_to([128, 2, 512]), ALU.mult,
               reads=[("bstage", hh), "mstage"], writes=[("EA", v, q)])

        def build_E(v, preloaded=False):
            if not preloaded:
                dma_sp(mstage, maskA_d[v], writes=["mstage"], grp="mst")
            for hh in range(2):
                bsl = hh
                if not preloaded:
                    dma_sp(bsth[bsl], biasA_d[v][:, hh * 4:(hh + 1) * 4, :], writes=[("bstage", bsl)], grp=f"bst{bsl}")
                act(bsth[bsl], bsth[bsl], AF.Exp, reads=[("bstage", bsl)], writes=[("bstage", bsl)])
                tt("dve", EA[:, v, hh * 4:(hh + 1) * 4, :], bsth[bsl], mstage.unsqueeze(1).broadcast_to([128, 4, 512]), ALU.mult,
                   reads=[("bstage", bsl), "mstage"], writes=[("EA", v, 2 * hh), ("EA", v, 2 * hh + 1)])
            if v == 0:
                dma_sp(bstB, biasB_d, writes=["bstB"], grp="mstB")
                dma_sp(mstB, maskB_d, writes=["mstB"], grp="mstB")
                act(bstB, bstB, AF.Exp, reads=["bstB"], writes=["bstB"])
                tt("dve", EB, bstB, mstB.unsqueeze(1).broadcast_to([128, 8, 64]), ALU.mult,
                   reads=["bstB", "mstB"], writes=["EB"])

        wf = A(Y0 + 18 * K + 512, 4 * 128, BF16).rearrange("p (g n) -> p g n", g=4)
        ccs = A(Y0 + 19 * K + 512, 256, BF16)
        AB = cs_rep[:].rearrange("p c m -> p (c m)").rearrange("p (g n) -> p g n", g=4)
        def load_AB():
            dma_sp(ccs[:, 0:128], cc_d, writes=["ccs"])
            dma_sp(ccs[:, 128:256], sc_d, writes=["ccs"])
            for g in range(4):
                dma_pool(wf[:, g, :], w_four[g], writes=[("wf", g)], grp=f"wf{g}")

        def build_AB():
            for g in range(4):
                for s2 in range(2):
                    mm(ps[:, 5, s2 * 128:(s2 + 1) * 128], ccs[:, s2 * 128:(s2 + 1) * 128], wf[:, g, :],
                       True, True, reads=["ccs", ("wf", g)], writes=[("ps", 5)])
                act(AB[:, g, :], ps[:, 5, 0:256], AF.Copy, xreads=[("ps", 5)], writes=["AB"])

        build_E0_quarters()

        TORDER = list(range(2, 14)) + [0, 1, 14, 15]
        useq = [(i, hp) for i in TORDER for hp in range(4)]
        tinfo = {i: (_attn_variant(i)[0], _attn_variant(i)[1], VARIANT_OF_TILE.get(i, 0)) for i in range(NT)}
        po = [ps[:, 6 + g, 0:260].rearrange("p (h d) -> p h d", h=4) for g in range(2)]

        def attn_qk(m):
            i, hp = useq[m]
            jb, extra, v = tinfo[i]
            u2 = m % 2
            qs = slice(i * 128, (i + 1) * 128)
            def qk_a(c, e):
                pb = e * 64
                mm(ps[:, u2 + 2 * e, c * 128:(c + 1) * 128], kT[pb:pb + 64, hp, (jb + c) * 128:(jb + c + 1) * 128],
                   qT[pb:pb + 64, hp, qs], True, True,
                   reads=[("kT", hp, (jb + c) // 4), ("qT", hp, i // 4)], writes=[("ps", u2 + 2 * e)])

            def qk_b(e):
                pb = e * 64
                mm(ps[:, 4 + u2, 256 + e * 64:256 + (e + 1) * 64], kT[pb:pb + 64, hp, (jb + 4) * 128:(jb + 5) * 128],
                   qT[pb:pb + 64, hp, i * 128 + 64:(i + 1) * 128], True, True,
                   reads=[("kT", hp, (jb + 4) // 4), ("qT", hp, i // 4)], writes=[("ps", 4 + u2)])

            for e in range(2):
                for c in range(4):
                    qk_a(c, e)
                if extra:
                    qk_b(e)

        def attn_soft(m):
            i, hp = useq[m]
            jb, extra, v = tinfo[i]
            u2 = m % 2
            for e in range(2):
                act(peA[u2][:, e, :], ps[:, u2 + 2 * e, :], AF.Exp, xreads=[("ps", u2 + 2 * e)], writes=[("peA", u2, e)])
                if e == 0 and extra:
                    act(peB[u2], ps[:, 4 + u2, 256:384].rearrange("p (e n) -> p e n", e=2), AF.Exp,
                        xreads=[("ps", 4 + u2)], writes=[("peB", u2)])
            for e in range(2):
                tt("dve", pTA[u2][:, e, :], peA[u2][:, e, :], EA[:, v, 2 * hp + e, :], ALU.mult,
                   reads=[("peA", u2, e), ("EA", v, hp)], writes=[("pTA", u2, e)])
                if e == 0 and extra:
                    tt("dve", pTB[u2], peB[u2], EB[:, 2 * hp:2 * hp + 2, :], ALU.mult, reads=[("peB", u2), "EB"], writes=[("pTB", u2)])

        def attn_pv(m):
            i, hp = useq[m]
            jb, extra, v = tinfo[i]
            u2 = m % 2
            nch = 5 if extra else 4
            for e in range(2):
                h = 2 * hp + e
                pv = po[h // 4]
                hb4 = h % 4
                for c in range(nch):
                    for qh in range(2):
                        if c == 4 and qh == 0:
                            continue
                        last = (c == 3 and not (extra and qh == 1)) or c == 4
                        if c < 4:
                            lhs = pTA[u2][:, e, c * 128 + qh * 64:c * 128 + (qh + 1) * 64]
                            rd = [("pTA", u2, e)]
                        else:
                            lhs = pTB[u2][:, e, :]
                            rd = [("pTB", u2)]
                        mm(pv[qh * 64:(qh + 1) * 64, hb4, :], lhs, Vext[:, jb + c, h, :], c == 0, last,
                           reads=rd + [("V", jb + c), "vones"], writes=[("ps", 6 + h // 4)])

        def attn_out_a(i):
            b2 = i % 2
            for g in range(2):
                recip(rden[:, b2, g * 4:(g + 1) * 4], po[g][:, :, 64], xreads=[("ps", 6 + g)], writes=[("rden", b2, g)])
                tt("dve", o_n[b2][:, g * 256:(g + 1) * 256].rearrange("p (h d) -> p h d", h=4), po[g][:, :, 0:64],
                   rden[:, b2, g * 4:(g + 1) * 4].unsqueeze(2).broadcast_to([128, 4, 64]), ALU.mult,
                   reads=[("rden", b2, g)], xreads=[("ps", 6 + g)], writes=[("o_n", b2, g)])

        def attn_out_a2(i):
            b2 = i % 2
            act(junk_b, o_n[b2], AF.Square, reads=[("o_n", b2, 0), ("o_n", b2, 1)], writes=["junk", ("ss", 1, i)],
                accum=ss[:, 1, i:i + 1])
            act(rstd[:, 1, i:i + 1], ss[:, 1, i:i + 1], AF.Ln, reads=[("ss", 1, i), "epsb"], writes=[("rstd", 1, i)],
                scale=1.0 / 512, bias=epsb[:, 0:1])
            act(rstd[:, 1, i:i + 1], rstd[:, 1, i:i + 1], AF.Exp, reads=[("rstd", 1, i)], writes=[("rstd", 1, i)], scale=-0.5)

        def attn_out_b(i):
            b2 = i % 2
            tsc("dve", o_bf[b2], o_n[b2], rstd[:, 1, i:i + 1], ALU.mult,
                reads=[("o_n", b2, 0), ("o_n", b2, 1), ("rstd", 1, i)], writes=[("o_bf", b2)])
            pst = ps[:, 4, :].bitcast(BF16)
            for c in range(4):
                tr(pst[:, c * 128:(c + 1) * 128], o_bf[b2][:, c * 128:(c + 1) * 128], reads=[("o_bf", b2)], writes=[("ps", 4)])

        def attn_out_c(i):
            pst = ps[:, 4, :].bitcast(BF16)
            cpy("dve", ynaT[:, :, i * 128:(i + 1) * 128], pst[:, 0:512].rearrange("p (c t) -> p c t", c=4),
                xreads=[("ps", 4)], writes=[("ynaT", i)])

        E_SCHED = {(4, 0): 1, (6, 0): 2, (8, 0): 3, (10, 0): 4}
        E_LOAD = {(3, 0): 1, (5, 0): 2, (7, 0): 3, (9, 0): 4}
        attn_qk(0)
        prev = None
        for m in range(len(useq)):
            if m + 1 < len(useq):
                attn_qk(m + 1)
            attn_soft(m)
            attn_pv(m)
            i, hp = useq[m]
            if hp == 3:
                attn_out_a(i)
            if prev is not None and hp == 0:
                attn_out_a2(prev)
            if prev is not None and hp == 1:
                attn_out_b(prev)
            if prev is not None and hp == 2:
                attn_out_c(prev)
            if hp == 3:
                prev = i
            if (i, hp) in E_LOAD:
                load_E(E_LOAD[(i, hp)])
            if (i, 0) in E_SCHED:
                build_E_quarter(E_SCHED[(i, 0)], hp)
            if m == 1:
                load_AB()
            if m == 8:
                build_AB()
        attn_out_a2(prev)

        PQ = A(64 * K, NT * 2 * 4 * 128, BF16).rearrange("p (t s g n) -> p t s g n", t=NT, s=2, g=4)
        Yst = A(0, 2 * 64 * NT * 8, BF16).rearrange("p (s j t e) -> p s j t e", s=2, j=64, t=NT)
        YT = A(32 * K, 64 * 2 * 128, BF16).rearrange("p (j s n) -> p j s n", j=64, s=2)
        yf = A(96 * K, NT * 512, BF16).rearrange("p (t n) -> p t n", t=NT)
        F0 = Y0 + 32 * K
        mct = A(F0 + 2 * K, 16 * 128, BF16).rearrange("p (t n) -> p t n", t=16)
        mst = A(F0 + 6 * K, 16 * 128, BF16).rearrange("p (t n) -> p t n", t=16)
        nmst = A(F0 + 10 * K, 16 * 128, BF16).rearrange("p (t n) -> p t n", t=16)
        bct = A(F0 + 14 * K, 128, BF16)
        bsnt = A(F0 + 14 * K + 256, 128, BF16)
        assert F0 + 14 * K + 512 <= TOP
        sqf = A(0, 512, F32)
        ybf = [A(2 * K + i * K, 512, BF16) for i in range(2)]
        sqf2 = [A(4 * K + i * 2 * K, 512, F32) for i in range(2)]


        def f_stage0(n2):
            for gp in range(2):
                bank = gp
                for gg in range(2):
                    g = 2 * gp + gg
                    mm(ps[:, bank, gg * 256:(gg + 1) * 256], uT[:, g, n2:S:16], AB[:, g, :], True, True,
                       reads=[("uT", g, tb) for tb in range(4)] + ["AB"], writes=[("ps", bank)])
                src = ps[:, bank, :].rearrange("p (g s n) -> p s g n", g=2, s=2)
                if gp == 0:
                    act(PQ[:, n2, :, 0:2, :], src, AF.Copy, xreads=[("ps", bank)], writes=[("PQ", n2, 0)])
                else:
                    cpy("dve", PQ[:, n2, :, 2:4, :], src, xreads=[("ps", bank)], writes=[("PQ", n2, 1)])

        def f_stage1(n2):
            pr = [("PQ", n2, 0), ("PQ", n2, 1), ("ftab", 0), ("ftab", 1), ("ftab", 2)]
            br = 2 + 2 * (n2 % 2)
            Pall = PQ[:, n2, 0, :, :]
            Qall = PQ[:, n2, 1, :, :]
            mm(ps[:, br, :], mct[:, n2, :], Pall, True, False, reads=pr, writes=[("ps", br)])
            mm(ps[:, br, :], nmst[:, n2, :], Qall, False, True, reads=pr, writes=[("ps", br)])
            mm(ps[:, br + 1, :], mst[:, n2, :], Pall, True, False, reads=pr, writes=[("ps", br + 1)])
            mm(ps[:, br + 1, :], mct[:, n2, :], Qall, False, True, reads=pr, writes=[("ps", br + 1)])
            act(Yst[:, 0, :, n2, :], ps[:, br, :].rearrange("p (j e) -> p j e", e=8), AF.Copy,
                xreads=[("ps", br)], writes=[("Y", n2, 0)])
            cpy("dve", Yst[:, 1, :, n2, :], ps[:, br + 1, :].rearrange("p (j e) -> p j e", e=8),
                xreads=[("ps", br + 1)], writes=[("Y", n2, 1)])

        f_stage0(0)
        f_stage0(1)
        attn_out_b(prev)
        f_stage0(2)
        f_stage0(3)
        attn_out_c(prev)
        dump("ynaT", ynaT, [128, 4, S], BF16, [("ynaT", i) for i in range(NT)])
        att_last = [P.ops[e][-1] for e in ("pe", "act", "dve")]
        for k_, (dst_, src_) in enumerate(((mct, mc_d), (mst, ms_d), (nmst, nms_d), (bct, bc_d), (bsnt, bsn_d))):
            P.add("sp", (lambda o, i_: (lambda e: e.dma_start(out=o, in_=i_)))(dst_, src_), writes=[("ftab", k_)], dma=f"ftab{k_}",
                  extra=att_last)
        for n2 in range(NT):
            if n2 + 4 < NT:
                f_stage0(n2 + 4)
            f_stage1(n2)

        YALL = [("Y", n2, s2) for n2 in range(NT) for s2 in range(2)]

        def f_transp(jb):
            bank = 6 + jb % 2
            pst = ps[:, bank, :].bitcast(BF16)
            for jj in range(4):
                j = 4 * jb + jj
                for s2 in range(2):
                    o = (jj * 2 + s2) * 128
                    tr(pst[:, o:o + 128], Yst[:, s2, j, :, :].rearrange("p t e -> p (t e)"), reads=YALL, writes=[("ps", bank)])
            src = pst.rearrange("p (j s n) -> p j s n", j=4, s=2)
            if jb % 2 == 0:
                act(YT[:, 4 * jb:4 * jb + 4, :, :], src, AF.Copy, xreads=[("ps", bank)], writes=[("YT", jb)])
            else:
                cpy("dve", YT[:, 4 * jb:4 * jb + 4, :, :], src, xreads=[("ps", bank)], writes=[("YT", jb)])

        def f_stage2(jb):
            bank = jb % 2
            for jj in range(4):
                j = 4 * jb + jj
                o = jj * 128
                mm(ps[:, bank, o:o + 128], YT[:, j, 0, :], bct, True, False, reads=[("YT", jb), ("ftab", 3), ("ftab", 4)], writes=[("ps", bank)])
                mm(ps[:, bank, o:o + 128], YT[:, j, 1, :], bsnt, False, True, reads=[("YT", jb), ("ftab", 3), ("ftab", 4)], writes=[("ps", bank)])
            src = ps[:, bank, :].rearrange("p (j k e) -> p k j e", j=4, k=16)
            dst = yf[:, :, 32 * jb:32 * jb + 32].rearrange("p k (j e) -> p k j e", j=4)
            if jb % 2 == 0:
                cpy("dve", dst, src, xreads=[("ps", bank)], writes=[("yf", jb)])
            else:
                act(dst, src, AF.Copy, xreads=[("ps", bank)], writes=[("yf", jb)])

        f_transp(0)
        for jb in range(16):
            if jb + 1 < 16:
                f_transp(jb + 1)
            f_stage2(jb)

        YFALL = [("yf", jb) for jb in range(16)]

        def f_norm_a(k2):
            if k2 >= NT:
                return
            if k2 % 2 == 0:
                act(sqf2[0], yf[:, k2, :], AF.Square, reads=YFALL, writes=[("sqf", 0), ("ss", 4, k2)], accum=ss[:, 4, k2:k2 + 1])
            else:
                tt("pool", sqf2[1], yf[:, k2, :], yf[:, k2, :], ALU.mult, reads=YFALL, writes=[("sqf", 1)])
                P.add("dve", lambda e: e.reduce_sum(out=ss[:, 4, k2:k2 + 1], in_=sqf2[1], axis=mybir.AxisListType.X),
                      reads=[("sqf", 1)], writes=[("ss", 4, k2)])
            rstd_of(4, k2, 512)

        def f_norm_b(k2):
            b2 = k2 % 2
            tsc("dve", ybf[b2], yf[:, k2, :], rstd[:, 4, k2:k2 + 1], ALU.mult, reads=YFALL + [("rstd", 4, k2)], writes=[("ybf", b2)])
            bank = 2 + b2
            pst = ps[:, bank, :].bitcast(BF16)
            for c in range(4):
                tr(pst[:, c * 128:(c + 1) * 128], ybf[b2][:, c * 128:(c + 1) * 128], reads=[("ybf", b2)], writes=[("ps", bank)])
            act(yfT[:, :, k2 * 128:(k2 + 1) * 128], pst[:, 0:512].rearrange("p (c t) -> p c t", c=4), AF.Copy,
                xreads=[("ps", bank)], writes=[("yfT", k2)])

        SP2 = S + 2
        N0 = ((8 * SP2 * 2 + 255) // 256) * 256
        wo = A(N0 + 6 * K, 8 * 1024, BF16).rearrange("p (c n) -> p c n", c=8)
        wst = [A(Y0 + 32 * K + i * 4 * K, 1024, F32) for i in range(2)]
        f_last = [P.ops[e][-1] for e in ("pe", "act", "dve")]

        def wo_dma(ec):
            if ec >= 8:
                return
            b2 = ec % 2
            P.add("sp", (lambda o, i_: (lambda e: e.dma_start(out=o, in_=i_)))(wst[b2], w_out[ec * 128:(ec + 1) * 128, :]),
                  writes=[("wst", b2)], dma=f"wst{b2}", extra=f_last)

        def wo_scale(ec):
            b2 = ec % 2
            P.add("dve", (lambda o, i0, sc: (lambda e: e.scalar_tensor_tensor(out=o, in0=i0, scalar=sc, in1=gt1, op0=ALU.mult, op1=ALU.mult)))(
                wo[:, ec, :], wst[b2], gout_col[:, ec:ec + 1]),
                reads=[("wst", b2), "gout_col"] + MOD["gt1"], writes=[("wo", ec)], extra=f_last)

        wo_dma(0)
        wo_dma(1)
        f_norm_a(0)
        f_norm_a(1)
        for k2 in range(NT):
            f_norm_a(k2 + 2)
            f_norm_b(k2)
            if 1 <= k2 < 9:
                wo_scale(k2 - 1)
                wo_dma(k2 + 1)
        dump("yfT", yfT, [128, 4, S], BF16, [("yfT", k2) for k2 in range(NT)])
        P.barrier()

        hT2 = A(0, 8 * SP2, BF16).rearrange("p (c t) -> p c t", c=8)
        xt4 = [A(N0 + 22 * K + i * 4 * K, 1024, F32) for i in range(2)]
        assert N0 + 30 * K <= 64 * K
        x2 = A(64 * K, NT * 1024, F32).rearrange("p (t n) -> p t n", t=NT)
        t1_c = [A(TOP + i * 4 * K, 1024, F32) for i in range(2)]
        hb_c = [A(Y0 + 40 * K + i * 2 * K, 1024, BF16) for i in range(2)]
        junk_c = A(Y0 + 44 * K, 1024, BF16)
        assert Y0 + 46 * K <= TOP
        mset("pool", hT2[:, :, 0:1], 0.0, ["h2pad"])
        mset("pool", hT2[:, :, SP2 - 1:SP2], 0.0, ["h2pad"])
        WR = 3
        wup = [A(N0 + i * 2 * K, 8 * 128, BF16).rearrange("p (c n) -> p c n", c=8) for i in range(WR)]
        for n_, c0_ in ((0, 0), (1, DFF)):
            dma_pool(wup[n_], w_up[:, c0_:c0_ + 128].rearrange("(c p) n -> p c n", p=128), writes=[("wup", n_)], grp=f"wup{n_}")

        def p5_s1(i):
            norm_s1(i, x2[:, i, :], [("x2", i, 0), ("x2", i, 1)], 2, G2, MOD["G2"], sh2, MOD["sh2"], t1_c, hb_c, junk_c)

        dma_sp(xt4[0], x[0:128, :], writes=[("xt4", 0)], grp="xt0")
        for i in range(NT):
            b3 = i % 2
            if i + 1 < NT:
                dma_sp(xt4[(i + 1) % 2], x[(i + 1) * 128:(i + 2) * 128, :], writes=[("xt4", (i + 1) % 2)], grp=f"xt{(i + 1) % 2}")
            for dh in range(2):
                bank = (2 * i + dh) % 4
                ds = slice(dh * 512, (dh + 1) * 512)
                for ec in range(8):
                    src = yfT if ec < 4 else ynaT
                    mm(ps[:, bank, :], src[:, ec % 4, i * 128:(i + 1) * 128], wo[:, ec, ds], ec == 0, ec == 7,
                       reads=[("wo", ec)], writes=[("ps", bank)])
                tt("dve", x2[:, i, ds], ps[:, bank, :], xt4[b3][:, ds], ALU.add,
                   reads=[("xt4", b3)], xreads=[("ps", bank)], writes=[("x2", i, dh)])
            if i >= 2:
                norm_s2(i - 2, hT2, "hT2", 1, hb_c, (6, 7))
            p5_s1(i)
        norm_s2(NT - 2, hT2, "hT2", 1, hb_c, (6, 7))
        norm_s2(NT - 1, hT2, "hT2", 1, hb_c, (6, 7))
        dump("x2", x2, [128, NT, 1024], F32, [("x2", i, dh) for i in range(NT) for dh in range(2)])
        dump("hT2", hT2, [128, 8, SP2], BF16, [("hT2", i) for i in range(NT)] + ["h2pad"])
        p5_last = [P.ops[e][-1] for e in ("pe", "act", "dve", "pool")]
        wdst = [A(N0 + 6 * K + i * 4 * K, 1024, F32) for i in range(2)]
        AC0 = N0 + 14 * K
        accg = [A(AC0 + i * 4352, 1026, F32).rearrange("p (b n) -> p b n", b=3) for i in range(2)]
        accv = [A(AC0 + 8704 + i * 4352, 1026, F32).rearrange("p (b n) -> p b n", b=3) for i in range(2)]
        assert AC0 + 4 * 4352 <= 64 * K, AC0
        junk_d = A(AC0, 1024, BF16)
        aT = [A(128 * K + i * 16 * K, 4 * S, BF16).rearrange("p (c t) -> p c t", c=4) for i in range(2)]
        wd = [A(160 * K + i * 8 * K, 4 * 1024, BF16).rearrange("p (c n) -> p c n", c=4) for i in range(2)]
        stage = [A(TOP + i * 2 * K, 512, F32) for i in range(2)]
        sg = [A(TOP + 4 * K + i * 2176, 1026, BF16).rearrange("p (b n) -> p b n", b=3) for i in range(2)]
        gfin = A(TOP + 12 * K, 1024, F32)

        _psz = [2, 4, 4, 4, 4, 4]
        fpieces = [list(range(sum(_psz[:k]), sum(_psz[:k + 1]))) for k in range(len(_psz))]
        assert fpieces[-1][-1] == NFC - 1
        NP = len(fpieces)
        OFFS = [0, 341, 682]
        NCH = 2 * NFC
        ucnt = [0]
        ecnt = [0]
        ccnt = [0]
        scnt = [0]

        def chunk_col(n):
            return (n // 2) * 128 if n % 2 == 0 else DFF + (n // 2) * 128

        def wup_load(n):
            return

        def wup_cast(n):
            if n >= NCH:
                return
            c0 = chunk_col(n)
            dma_pool(wup[n % WR], w_up[:, c0:c0 + 128].rearrange("(c p) n -> p c n", p=128), writes=[("wup", n % WR)], grp=f"wup{n % WR}")


        def up_unit(wb, hf):
            u = ucnt[0] % 2
            ucnt[0] += 1
            rds = [("wup", wb), "h2pad"] + [("hT2", t) for t in range(8 * hf, 8 * hf + 8)] + [("hT2", 8 if hf == 0 else 7)]
            for b in range(3):
                bank = 3 * u + b
                c0 = hf * 1024 + OFFS[b]
                for kc in range(8):
                    mm(ps[:, bank, 0:344], wup[wb][:, kc, :], hT2[:, kc, c0:c0 + 344], kc == 0, kc == 7, reads=rds,
                       writes=[("psu", u), ("ps", bank)])
            return u

        def conv_unit(u, ch, acc, accname):
            pu = ps[:, 3 * u:3 * u + 3, :]
            ccnt[0] += 1
            P.add("act", (lambda o, i_, sc, bi: (lambda e: e.activation(out=o, in_=i_, func=AF.Identity, scale=sc, bias=bi)))(
                acc, pu[:, :, 1:343], convw[:, ch, 1:2], convb[:, ch:ch + 1]),
                reads=["convw", "convb"], xreads=[("psu", u)], writes=[accname], extra=(p5_last if ccnt[0] <= 4 else ()))
            stt("dve", acc, pu[:, :, 0:342], convw[:, ch, 0:1], acc, ALU.mult, ALU.add,
                reads=["convw", accname], xreads=[("psu", u)], writes=[accname])
            stt("dve", acc, pu[:, :, 2:344], convw[:, ch, 2:3], acc, ALU.mult, ALU.add,
                reads=["convw", accname], xreads=[("psu", u)], writes=[accname])

        pending = []

        def drain(k):
            for _ in range(k):
                if pending:
                    pending.pop(0)()

        def up_piece(pi, ndrain):
            ab = pi % 2
            for ci, j in enumerate(fpieces[pi]):
                wg = (2 * j) % WR
                wv = (2 * j + 1) % WR
                for hf in range(2):
                    k2 = hf
                    ug = up_unit(wg, hf)
                    conv_unit(ug, j, accg[k2], ("accg", k2))
                    wup_cast(2 * j + 2 + hf)
                    wup_load(2 * j + 4 + hf)
                    if ci == len(fpieces[pi]) - 1:
                        wd_dma(pi + 1, hf)
                    drain(ndrain)
                    uv = up_unit(wv, hf)
                    conv_unit(uv, NFC + j, accv[k2], ("accv", k2))
                    drain(ndrain)
                    act(sg[k2], accg[k2], AF.Silu, reads=[("accg", k2)], writes=[("sg", k2)])
                    base = hf * 1024
                    for b in range(3):
                        P.add("dve", (lambda o, i0, i1: (lambda e: e.tensor_tensor(out=o, in0=i0, in1=i1, op=ALU.mult)))(
                            aT[ab][:, ci, base + OFFS[b]:base + OFFS[b] + 342], sg[k2][:, b, :], accv[k2][:, b, :]),
                            reads=[("sg", k2), ("accv", k2)], writes=[("aT", ab, ci, hf)], extra=(p5_last if pi == 1 else ()))

        wd_pref = set()

        def wd_dma(pi, ci):
            if pi >= NP or ci >= len(fpieces[pi]) or (pi, ci) in wd_pref:
                return
            wd_pref.add((pi, ci))
            ex = p5_last if pi < 2 else ()
            j = fpieces[pi][ci]
            b2 = ci % 2
            P.add("sp", (lambda o, i_: (lambda e: e.dma_start(out=o, in_=i_)))(wdst[b2], w_down[j * 128:(j + 1) * 128, :]),
                  writes=[("wdst", b2)], dma=f"wdst{b2}", extra=ex)

        def load_wd(pi):
            ab = pi % 2
            ex = p5_last if pi < 2 else ()
            wd_dma(pi, 0)
            wd_dma(pi, 1)
            for ci, j in enumerate(fpieces[pi]):
                b2 = ci % 2
                P.add("pool", (lambda o, i0: (lambda e: e.tensor_tensor(out=o, in0=i0, in1=gt2, op=ALU.mult)))(wd[ab][:, ci, :], wdst[b2]),
                      reads=[("wdst", b2)] + MOD["gt2"], writes=[("wd", ab, ci)], extra=ex)
                wd_dma(pi, ci + 2)

        def down_group(pi, t, dh):
            ab = pi % 2
            n = len(fpieces[pi])
            bank = 6 + dh
            ds = slice(dh * 512, (dh + 1) * 512)
            for ci in range(n):
                mm(ps[:, bank, :], aT[ab][:, ci, t * 128:(t + 1) * 128], wd[ab][:, ci, ds], ci == 0, ci == n - 1,
                   reads=[("aT", ab, ci, t // 8), ("wd", ab, ci)], writes=[("ps", bank)])
            if dh == 1 and (pi != NP - 2 or t % 2 == 1):
                tt("dve", x2[:, t, ds], ps[:, bank, :], x2[:, t, ds], ALU.add,
                   reads=[("x2", t, dh)], xreads=[("ps", bank)], writes=[("x2", t, dh)])
            else:
                sb2 = scnt[0] % 2
                scnt[0] += 1
                act(stage[sb2], ps[:, bank, :], AF.Copy, xreads=[("ps", bank)], writes=[("stage", sb2)])
                tt("pool", x2[:, t, ds], x2[:, t, ds], stage[sb2], ALU.add,
                   reads=[("stage", sb2), ("x2", t, dh)], writes=[("x2", t, dh)])
            if pi == NP - 1 and dh == 1:
                if t > 0:
                    final_norm(t - 1)
                if t == NT - 1:
                    final_norm(t)

        def final_norm(t):
            xr = [("x2", t, 0), ("x2", t, 1)]
            act(junk_d, x2[:, t, :], AF.Square, reads=xr, writes=["junk", ("ss", 3, t)], accum=ss[:, 3, t:t + 1])
            rstd_of(3, t, D)
            stt("dve", x2[:, t, :], x2[:, t, :], rstd[:, 3, t:t + 1], gfin, ALU.mult, ALU.mult,
                reads=xr + [("rstd", 3, t), "gfin"], writes=xr)
            P.add("sp", lambda e: e.dma_start(out=y_out[t * 128:(t + 1) * 128, :], in_=x2[:, t, :]), reads=xr, dma="out")

        def queue_down(pi):
            for t in range(NT):
                for dh in range(2):
                    pending.append(lambda pi=pi, t=t, dh=dh: down_group(pi, t, dh))

        load_wd(0)
        P.add("sp", lambda e: e.dma_start(out=gfin, in_=g_fin_bc), writes=["gfin"], dma="gfin", extra=p5_last)
        up_piece(0, 0)
        for pi in range(1, NP):
            load_wd(pi)
            queue_down(pi - 1)
            nunits = 4 * len(fpieces[pi])
            up_piece(pi, -(-32 // nunits))
            drain(len(pending))
        queue_down(NP - 1)
        drain(len(pending))

        P.emit(nc, final_dma_groups=["out"])
    return nc, dbg


_NC_CACHE = None


def kernel(**inputs):
    global _NC_CACHE
    inp = {k: np.asarray(v) for k, v in inputs.items()}
    maps = _layout_inputs(inp)
    if _NC_CACHE is None:
        _NC_CACHE = build_nc()
    nc, dbg = _NC_CACHE
    res = run_bass_kernel_spmd(nc, maps, core_ids=list(range(8)))
    out = np.stack([np.asarray(res.results[b]["y"], dtype=np.float32) for b in range(8)], axis=0)
    if DEBUG:
        kernel.debug = [{k: np.asarray(res.results[b]["dbg_" + k]) for k in dbg} for b in range(8)]
    return out
```
